# Optimizing a Trainium2 kernel written in Bass

```python
import math
import jax
import jax.numpy as jnp
from jax import lax
import numpy as np

D_MODEL = 1024
BATCH = 8
SEQ = 4096
DEPTH = 2

GRID_W = 64
CTX_LEN = 256
N_MIXERS = 2
N_SSD_LAYERS = (DEPTH + N_MIXERS - 1) // N_MIXERS
N_GM_LAYERS = DEPTH // N_MIXERS
N_MOD = 9
MACARON_W = 0.5
EPS = 1e-6

FFN_DIM = 2816

SSD_INNER = 2 * D_MODEL
SSD_HEAD_DIM = 64
SSD_HEADS = SSD_INNER // SSD_HEAD_DIM
SSD_GROUPS = 8
SSD_HPG = SSD_HEADS // SSD_GROUPS
SSD_STATE = 128
SSD_CONV = 5
SSD_CHUNK = 128
SSD_CONV_DIM = SSD_INNER + 2 * SSD_GROUPS * SSD_STATE
SSD_PROJ = SSD_INNER + SSD_CONV_DIM + 2 * SSD_HEADS

GM_CHUNK = 128
GM_INNER = 2 * D_MODEL
GM_GROUPS = 8
GM_GROUP_DIM = GM_INNER // GM_GROUPS

kernel_name = 'hybrid_ssd_gmlp_macaron_dit_block'


def rms_norm(x, g):
    xf = x.astype(jnp.float32)
    y = xf * lax.rsqrt(jnp.mean(xf * xf, axis=-1, keepdims=True) + EPS)
    return (y * g.astype(jnp.float32)).astype(x.dtype)


def layer_norm(x, g, b):
    xf = x.astype(jnp.float32)
    mu = jnp.mean(xf, axis=-1, keepdims=True)
    var = jnp.mean(jnp.square(xf - mu), axis=-1, keepdims=True)
    y = (xf - mu) * lax.rsqrt(var + EPS) * g.astype(jnp.float32) + b.astype(jnp.float32)
    return y.astype(x.dtype)


def sublayer_in(h, g_pre, shift, scale):
    return rms_norm(h, g_pre) * (1 + scale) + shift


def sublayer_out(h, y, g_post, gate, weight):
    return h + weight * gate * rms_norm(y, g_post)


def swiglu(h, w_in, w_out):
    gate, up = jnp.split(h @ w_in, 2, axis=-1)
    return (jax.nn.silu(gate) * up) @ w_out


def depthwise_conv(u, w, b):
    y = lax.conv_general_dilated(
        u, w[:, None, :].astype(u.dtype), window_strides=(1,),
        padding=[(SSD_CONV // 2, SSD_CONV // 2)],
        dimension_numbers=('NWC', 'WIO', 'NWC'),
        feature_group_count=u.shape[-1])
    return y + b


def ssd_chunked(xh, dt, A, Bm, Cm, h0):
    b, l, g, k, p = xh.shape
    n = Bm.shape[-1]
    q = SSD_CHUNK
    nc = l // q
    xc = xh.reshape(b, nc, q, g, k, p)
    dtc = dt.reshape(b, nc, q, g, k)
    Bc = Bm.reshape(b, nc, q, g, n)
    Cc = Cm.reshape(b, nc, q, g, n)
    a_cum = jnp.cumsum(dtc * A, axis=2)
    xdt = xc * dtc[..., None]
    tri = jnp.tril(jnp.ones((q, q), dtype=bool))[None, None, :, :, None, None]
    seg = a_cum[:, :, :, None] - a_cum[:, :, None, :]
    decay = jnp.exp(jnp.where(tri, seg, -jnp.inf))
    cb = jnp.einsum('bcign,bcjgn->bcijg', Cc, Bc)
    y_diag = jnp.einsum('bcijg,bcijgk,bcjgkp->bcigkp', cb, decay, xdt)
    decay_end = jnp.exp(a_cum[:, :, -1:] - a_cum)
    states = jnp.einsum('bcqgn,bcqgk,bcqgkp->bcgkpn', Bc, decay_end, xdt)
    chunk_decay = jnp.exp(a_cum[:, :, -1])

    def step(h, inp):
        s, d = inp
        return h * d[..., None, None] + s, h

    h_last, h_start = lax.scan(step, h0, (jnp.moveaxis(states, 1, 0), jnp.moveaxis(chunk_decay, 1, 0)))
    h_start = jnp.moveaxis(h_start, 0, 1)
    y_off = jnp.einsum('bcign,bcgkpn,bcigk->bcigkp', Cc, h_start, jnp.exp(a_cum))
    return (y_diag + y_off).reshape(b, l, g, k, p), h_last


def ssd_branch(u, h0f, h0b, w_in, conv_w, conv_b, dt_bias, A_log, D_skip, norm_g, w_out):
    bsz, l, _ = u.shape
    proj = u @ w_in
    z = proj[..., :SSD_INNER]
    xbc = proj[..., SSD_INNER:SSD_INNER + SSD_CONV_DIM]
    dt_raw = proj[..., SSD_INNER + SSD_CONV_DIM:]
    xbc = jax.nn.silu(depthwise_conv(xbc, conv_w, conv_b)).astype(jnp.float32)
    gn = SSD_GROUPS * SSD_STATE
    xs = xbc[..., :SSD_INNER].reshape(bsz, l, SSD_GROUPS, SSD_HPG, SSD_HEAD_DIM)
    Bm = xbc[..., SSD_INNER:SSD_INNER + gn].reshape(bsz, l, SSD_GROUPS, SSD_STATE)
    Cm = xbc[..., SSD_INNER + gn:].reshape(bsz, l, SSD_GROUPS, SSD_STATE)
    dt = jax.nn.softplus(dt_raw.astype(jnp.float32).reshape(bsz, l, 2, SSD_GROUPS, SSD_HPG)
                         + dt_bias.astype(jnp.float32).reshape(2, SSD_GROUPS, SSD_HPG))
    A = -jnp.exp(A_log.astype(jnp.float32)).reshape(2, SSD_GROUPS, SSD_HPG)
    rev = lambda t: jnp.flip(t, axis=1)
    yf, hf = ssd_chunked(xs, dt[:, :, 0], A[0], Bm, Cm, h0f)
    yb, hb = ssd_chunked(rev(xs), rev(dt[:, :, 1]), A[1], rev(Bm), rev(Cm), h0b)
    y = yf + rev(yb) + D_skip.astype(jnp.float32).reshape(SSD_GROUPS, SSD_HPG, 1) * xs
    y = y.reshape(bsz, l, SSD_INNER) * jax.nn.silu(z.astype(jnp.float32))
    y = y.reshape(bsz, l, SSD_GROUPS, SSD_INNER // SSD_GROUPS)
    y = y * lax.rsqrt(jnp.mean(y * y, axis=-1, keepdims=True) + EPS)
    y = y.reshape(bsz, l, SSD_INNER) * norm_g.astype(jnp.float32)
    return y.astype(u.dtype) @ w_out, hf, hb


def gmlp_branch(u, w_in, v_g, v_b, w_s, b_s, w_out):
    bsz, l, _ = u.shape
    gu, gv = jnp.split(jax.nn.gelu(u @ w_in), 2, axis=-1)
    gv = layer_norm(gv, v_g, v_b).reshape(bsz, l // GM_CHUNK, GM_CHUNK, GM_GROUPS, GM_GROUP_DIM)
    s = jnp.einsum('gij,bcjgd->bcigd', w_s, gv) + b_s.T[:, :, None]
    return (gu * s.reshape(bsz, l, GM_INNER)) @ w_out


def setup_inputs(seed: int = 0) -> dict:
    key = jax.random.key(seed)
    ks = jax.random.split(key, 24)
    D = D_MODEL

    def nrm(k, shape, scale):
        return jax.random.normal(k, shape, jnp.float32) * scale

    lo, hi = math.log(1e-3), math.log(1e-1)
    dt0 = jnp.exp(jax.random.uniform(ks[12], (N_SSD_LAYERS, 2, SSD_HEADS), jnp.float32, lo, hi))
    return {
        'x': nrm(ks[0], (BATCH, SEQ, D), 1.0),
        'c': nrm(ks[1], (BATCH, D), 1.0),
        'ctx': nrm(ks[2], (BATCH, CTX_LEN, D), 1.0),
        'c_ctx': nrm(ks[3], (D,), 1.0),
        'ada_w': nrm(ks[4], (DEPTH, D, N_MOD * D), 0.5 * D ** -0.5),
        'ada_b': nrm(ks[5], (DEPTH, N_MOD * D), 0.02),
        'norm_g': 1.0 + nrm(ks[6], (DEPTH, 6, D), 0.02),
        'ffn_w_in': nrm(ks[7], (DEPTH, 2, D, 2 * FFN_DIM), D ** -0.5),
        'ffn_w_out': nrm(ks[8], (DEPTH, 2, FFN_DIM, D), FFN_DIM ** -0.5),
        'ssd_w_in': nrm(ks[9], (N_SSD_LAYERS, D, SSD_PROJ), D ** -0.5),
        'ssd_conv_w': nrm(ks[10], (N_SSD_LAYERS, SSD_CONV, SSD_CONV_DIM), SSD_CONV ** -0.5),
        'ssd_conv_b': nrm(ks[11], (N_SSD_LAYERS, SSD_CONV_DIM), 0.02),
        'ssd_dt_bias': dt0 + jnp.log(-jnp.expm1(-dt0)),
        'ssd_A_log': jnp.log(jax.random.uniform(ks[13], (N_SSD_LAYERS, 2, SSD_HEADS), jnp.float32, 1.0, 16.0)),
        'ssd_D': 1.0 + nrm(ks[14], (N_SSD_LAYERS, SSD_HEADS), 0.02),
        'ssd_norm_g': 1.0 + nrm(ks[15], (N_SSD_LAYERS, SSD_INNER), 0.02),
        'ssd_w_out': nrm(ks[16], (N_SSD_LAYERS, SSD_INNER, D), SSD_INNER ** -0.5),
        'gm_w_in': nrm(ks[17], (N_GM_LAYERS, D, 2 * GM_INNER), D ** -0.5),
        'gm_v_g': 1.0 + nrm(ks[18], (N_GM_LAYERS, GM_INNER), 0.02),
        'gm_v_b': nrm(ks[19], (N_GM_LAYERS, GM_INNER), 0.02),
        'gm_w_s': nrm(ks[20], (N_GM_LAYERS, GM_GROUPS, GM_CHUNK, GM_CHUNK), GM_CHUNK ** -0.5),
        'gm_b_s': 1.0 + nrm(ks[21], (N_GM_LAYERS, GM_GROUPS, GM_CHUNK), 0.02),
        'gm_w_out': nrm(ks[22], (N_GM_LAYERS, GM_INNER, D), GM_INNER ** -0.5),
    }


def reference(x, c, ctx, c_ctx, ada_w, ada_b, norm_g, ffn_w_in, ffn_w_out,
              ssd_w_in, ssd_conv_w, ssd_conv_b, ssd_dt_bias, ssd_A_log, ssd_D, ssd_norm_g, ssd_w_out,
              gm_w_in, gm_v_g, gm_v_b, gm_w_s, gm_b_s, gm_w_out):
    bsz = x.shape[0]
    silu_c = jax.nn.silu(c)
    silu_cc = jax.nn.silu(c_ctx)
    for i in range(DEPTH):
        use_ssd = (i % N_MIXERS) == 0
        j = i // N_MIXERS
        last = i == DEPTH - 1
        ctx_needed = (not last) or use_ssd
        ctx_full = not last
        mx = jnp.split((silu_c @ ada_w[i] + ada_b[i])[:, None, :], N_MOD, axis=-1)
        mc = jnp.split(silu_cc @ ada_w[i] + ada_b[i], N_MOD, axis=-1)
        g = norm_g[i]

        x = sublayer_out(x, swiglu(sublayer_in(x, g[0], mx[0], mx[1]), ffn_w_in[i, 0], ffn_w_out[i, 0]),
                         g[1], mx[2], MACARON_W)
        if ctx_needed:
            ctx = sublayer_out(ctx, swiglu(sublayer_in(ctx, g[0], mc[0], mc[1]), ffn_w_in[i, 0], ffn_w_out[i, 0]),
                               g[1], mc[2], MACARON_W)

        xm = sublayer_in(x, g[2], mx[3], mx[4])
        if use_ssd:
            cm = sublayer_in(ctx, g[2], mc[3], mc[4])
            h0 = jnp.zeros((bsz, SSD_GROUPS, SSD_HPG, SSD_HEAD_DIM, SSD_STATE), jnp.float32)
            y_ctx, hf, hb = ssd_branch(cm, h0, h0, ssd_w_in[j], ssd_conv_w[j], ssd_conv_b[j], ssd_dt_bias[j],
                                       ssd_A_log[j], ssd_D[j], ssd_norm_g[j], ssd_w_out[j])
            y_x, _, _ = ssd_branch(xm, hf, hb, ssd_w_in[j], ssd_conv_w[j], ssd_conv_b[j], ssd_dt_bias[j],
                                   ssd_A_log[j], ssd_D[j], ssd_norm_g[j], ssd_w_out[j])
        else:
            y_x = gmlp_branch(xm, gm_w_in[j], gm_v_g[j], gm_v_b[j], gm_w_s[j], gm_b_s[j], gm_w_out[j])
            if ctx_full:
                cm = sublayer_in(ctx, g[2], mc[3], mc[4])
                y_ctx = gmlp_branch(cm, gm_w_in[j], gm_v_g[j], gm_v_b[j], gm_w_s[j], gm_b_s[j], gm_w_out[j])
        x = sublayer_out(x, y_x, g[3], mx[5], 1.0)

        x = sublayer_out(x, swiglu(sublayer_in(x, g[4], mx[6], mx[7]), ffn_w_in[i, 1], ffn_w_out[i, 1]),
                         g[5], mx[8], MACARON_W)
        if ctx_full:
            ctx = sublayer_out(ctx, y_ctx, g[3], mc[5], 1.0)
            ctx = sublayer_out(ctx, swiglu(sublayer_in(ctx, g[4], mc[6], mc[7]), ffn_w_in[i, 1], ffn_w_out[i, 1]),
                               g[5], mc[8], MACARON_W)
    return x
```

```python
import numpy as np
from contextlib import ExitStack
import concourse.bass as bass
import concourse.mybir as mybir
from concourse.bass_utils import run_bass_kernel_spmd

F32 = mybir.dt.float32
BF16 = mybir.dt.bfloat16
AF = mybir.ActivationFunctionType
ALU = mybir.AluOpType
AX = mybir.AxisListType

D = 1024
FF = 2816
NJ = FF // 128
EPS = 1e-6
NCORES = 8
LNEXP = False
import os
XV = int(os.environ.get("XV", "3"))
GV = int(os.environ.get("GV", "0"))
BGCAST = int(os.environ.get("BGCAST", "1"))
SCHED_MODE = 9


class Buf:
    __slots__ = ("ap", "name", "sem", "cnt", "kind")

    def __init__(self, ap, name=""):
        self.ap = ap
        self.name = name
        self.sem = None
        self.cnt = 0
        self.kind = None


def _elems(ap):
    n = 1
    for d in ap.shape[1:]:
        n *= int(d)
    return n


class Prog:
    ENG = ("pe", "act", "dve", "pool", "sp")

    def __init__(self, nc, es):
        self.nc = nc
        self.es = es
        self.eng = {"pe": nc.tensor, "act": nc.scalar, "dve": nc.vector, "pool": nc.gpsimd, "sp": nc.sync}
        self.ops = []
        self.esem = {e: es.enter_context(nc.semaphore("s_" + e)) for e in ("pe", "act", "dve", "pool")}
        self.bufs = []
        self.nsem = 0
        self.sempool = {"sw": [], "hw": []}
        self.dirty = set()
        self.nsb = 0

    def buf(self, ap, name=""):
        b = Buf(ap, name)
        self.bufs.append(b)
        return b

    def sb(self, stack, name, shape, dt):
        self.nsb += 1
        return stack.enter_context(self.nc.sbuf_tensor("%s_u%d" % (name, self.nsb), list(shape), dt))

    def _bsem(self, b, q):
        kind = "sw" if q == "pool" else "hw"
        if b.sem is not None:
            assert b.kind == kind, "buffer %s mixes SW and HW DMA queues" % b.name
        if b.sem is None:
            b.kind = kind
            if self.sempool[kind]:
                b.sem, b.cnt = self.sempool[kind].pop()
            else:
                b.sem = self.es.enter_context(self.nc.semaphore("b%d" % self.nsem))
                self.nsem += 1
        return b.sem

    def mark(self):
        return len(self.bufs)

    def release(self, mark):
        for b in self.bufs[mark:]:
            if b.sem is not None:
                self.sempool[b.kind].append((b.sem, b.cnt))
                b.sem = None

    def op(self, eng, meth, *args, reads=(), writes=(), **kw):
        out = kw.get("out", args[0] if args else None)
        n = _elems(out)
        if eng == "pe":
            fp32 = len(args) > 1 and args[1].dtype == F32
            dur = 0.03 + n * (4 if fp32 else 1) / 2400.0
        elif eng == "act":
            dur = 0.25 + n / 1150.0
        elif eng == "dve":
            dur = 0.08 + n / 900.0
        else:
            dur = 0.15 + n / 420.0
        grp = eng == "pe" and kw.get("start", True) is False
        self.ops.append(["c", eng, (meth, args, kw), list(reads), list(writes), dur, grp, None, None])

    def dma(self, q, out, in_, reads=(), writes=(), sig=None, bg=False, after=None):
        sb = sig if sig is not None else (writes[0] if writes else reads[0])
        sem = self._bsem(sb, q)
        sb.cnt += 16
        if not bg:
            self.dirty.add(sb)
        nbytes = 1
        for d in out.shape:
            nbytes *= int(d)
        nbytes *= 2 if out.dtype == BF16 else 4
        lat = 2.0 + nbytes / 150e3
        self.ops.append(["d", q, (out, in_), list(reads), list(writes), 1.0 if q == "pool" else 0.12, False, (sem, sb.cnt), lat, after])

    def barrier(self):
        self.ops.append(["bar", None, [(b.sem, b.cnt) for b in self.dirty]])
        self.dirty = set()

    def _schedule(self, ops):
        import heapq
        n = len(ops)
        unit_of = [0] * n
        units = []
        for i, o in enumerate(ops):
            if o[6] and units and ops[units[-1][-1]][1] == "pe" and units[-1][-1] == self._last_pe:
                units[-1].append(i)
            else:
                units.append([i])
            unit_of[i] = len(units) - 1
            if o[1] == "pe":
                self._last_pe = i
        nu = len(units)
        preds = [dict() for _ in range(nu)]
        sync = [set() for _ in range(n)]
        lw, rd, lastsem = {}, {}, {}
        for i, o in enumerate(ops):
            kind, eng, _, reads, writes = o[0], o[1], o[2], o[3], o[4]
            ui = unit_of[i]

            def edge(j, need_sync):
                uj = unit_of[j]
                if uj != ui:
                    pj = ops[j]
                    lat = (pj[8] if pj[0] == "d" else pj[5]) if True else 0.0
                    preds[ui][uj] = max(preds[ui].get(uj, 0.0), 0.2 if need_sync else 0.0)
                if need_sync and j != i:
                    sync[i].add(j)

            for b in reads:
                j = lw.get(id(b))
                if j is not None:
                    pj = ops[j]
                    edge(j, pj[0] == "d" or kind == "d" or pj[1] != eng or eng != "pe")
            for b in writes:
                j = lw.get(id(b))
                if j is not None:
                    pj = ops[j]
                    edge(j, pj[0] == "d" or kind == "d" or pj[1] != eng or eng != "pe")
                for j in rd.get(id(b), ()):
                    pj = ops[j]
                    edge(j, pj[0] == "d" or kind == "d" or pj[1] != eng or eng != "pe")
            if kind == "d":
                j = lastsem.get(o[7][0])
                if j is not None:
                    edge(j, False)
                lastsem[o[7][0]] = i
            for b in reads:
                rd.setdefault(id(b), []).append(i)
            for b in writes:
                lw[id(b)] = i
                rd[id(b)] = []
        pinned = {2: ("pe",), 3: ("sp",), 4: ("pe", "sp"), 5: ("act", "dve", "pool"), 6: ("pe", "sp", "pool"), 7: ("pool",), 8: ("act",), 9: ("dve",), 10: ("act", "pool"), 11: ("dve", "pool"), 12: ("act", "dve")}.get(SCHED_MODE, ())
        lastu = {}
        for u in range(nu):
            e = ops[units[u][0]][1]
            if e in pinned:
                if e in lastu and lastu[e] not in preds[u]:
                    preds[u][lastu[e]] = 0.0
                lastu[e] = u
        succs = [[] for _ in range(nu)]
        npred = [0] * nu
        for u in range(nu):
            npred[u] = len(preds[u])
            for p in preds[u]:
                succs[p].append(u)
        udur = [sum(ops[i][5] for i in units[u]) for u in range(nu)]
        ueng = [ops[units[u][0]][1] for u in range(nu)]
        ulat = [ops[units[u][-1]][8] if ops[units[u][-1]][0] == "d" else None for u in range(nu)]
        fin = [0.0] * nu
        ready = {e: [] for e in self.ENG}
        avail = {e: [] for e in self.ENG}
        free = {e: 0.0 for e in self.ENG}
        for u in range(nu):
            if npred[u] == 0:
                heapq.heappush(ready[ueng[u]], (0.0, u))
        order = []
        left = nu
        while left:
            best = None
            for e in self.ENG:
                while ready[e] and ready[e][0][0] <= free[e]:
                    heapq.heappush(avail[e], heapq.heappop(ready[e])[1])
                if avail[e]:
                    cand = (free[e], avail[e][0], e, True)
                elif ready[e]:
                    cand = (ready[e][0][0], ready[e][0][1], e, False)
                else:
                    continue
                if best is None or cand[:2] < best[:2]:
                    best = cand
            if SCHED_MODE == 0:
                order = list(range(n))
                break
            st, u, e, from_avail = best
            if from_avail:
                heapq.heappop(avail[e])
            else:
                heapq.heappop(ready[e])
            free[e] = st + udur[u]
            fin[u] = st + (ulat[u] if ulat[u] is not None else udur[u])
            order.extend(units[u])
            left -= 1
            for v in succs[u]:
                npred[v] -= 1
                if npred[v] == 0:
                    rt = max(fin[p] + lat for p, lat in preds[v].items())
                    heapq.heappush(ready[ueng[v]], (rt, v))
        return order, sync

    def emit(self):
        cnt = {e: 0 for e in self.esem}
        waited_c = {e: {} for e in self.eng}
        waited_d = {e: {} for e in self.eng}
        phases, cur = [], []
        for o in self.ops:
            if o[0] == "bar":
                phases.append((cur, o[2]))
                cur = []
            else:
                cur.append(o)
        if cur:
            phases.append((cur, []))
        self._last_pe = -1
        for ops, bar_dma in phases:
            self._last_pe = -1
            order, sync = self._schedule(ops)
            sig = set()
            for i in range(len(ops)):
                for j in sync[i]:
                    if ops[j][0] == "c":
                        sig.add(j)
            last = {}
            for i in order:
                if ops[i][0] == "c":
                    last[ops[i][1]] = i
            sig.update(last.values())
            count_of = {}
            for i in order:
                if ops[i][0] == "c" and i in sig:
                    cnt[ops[i][1]] += 1
                    count_of[i] = cnt[ops[i][1]]
            for i in order:
                o = ops[i]
                e = o[1]
                eng = self.eng[e]
                cd, dd = {}, {}
                for j in sync[i]:
                    pj = ops[j]
                    if pj[0] == "c":
                        cd[pj[1]] = max(cd.get(pj[1], 0), count_of[j])
                    else:
                        sm_, v = pj[7]
                        dd[sm_] = max(dd.get(sm_, 0), v)
                if o[0] == "d" and o[9] is not None:
                    dd[o[9][0]] = max(dd.get(o[9][0], 0), o[9][1])
                for f, v in cd.items():
                    if waited_c[e].get(f, 0) >= v:
                        continue
                    waited_c[e][f] = v
                    eng.wait_ge(self.esem[f], v)
                for sm_, v in dd.items():
                    if waited_d[e].get(sm_, 0) >= v:
                        continue
                    waited_d[e][sm_] = v
                    eng.wait_ge(sm_, v)
                if o[0] == "c":
                    meth, args, kw = o[2]
                    ins = getattr(eng, meth)(*args, **kw)
                    if i in sig:
                        ins.then_inc(self.esem[e], 1)
                else:
                    out, in_ = o[2]
                    eng.dma_start(out=out, in_=in_).then_inc(o[7][0], 16)
            for e in self.eng:
                eng = self.eng[e]
                for f in self.esem:
                    v = cnt[f]
                    if v > 0 and waited_c[e].get(f, 0) < v:
                        waited_c[e][f] = v
                        eng.wait_ge(self.esem[f], v)
                for sm_, v in bar_dma:
                    if waited_d[e].get(sm_, 0) < v:
                        waited_d[e][sm_] = v
                        eng.wait_ge(sm_, v)


class PsumRing:
    def __init__(self, P, es):
        nc = P.nc
        self.t = es.enter_context(nc.psum_tensor("psum", [128, 8, 512], F32))
        self.banks = [P.buf(self.t[:, i, :], "ps%d" % i) for i in range(8)]
        self.i = 0

    def one(self):
        b = self.banks[self.i]
        k = self.i
        self.i = (self.i + 1) % 8
        return b, self.t[:, k, :]

    def two(self):
        if self.i % 2:
            self.i = (self.i + 1) % 8
        k = self.i
        self.i = (self.i + 2) % 8
        return [self.banks[k], self.banks[k + 1]], self.t[:, k:k + 2, :]


def run_pipeline(gens, interval):
    active = []
    nxt = 0
    r = 0
    while nxt < len(gens) or active:
        if nxt < len(gens) and r % interval == 0:
            active.append(gens[nxt])
            nxt += 1
        for g in list(active):
            try:
                next(g)
            except StopIteration:
                active.remove(g)
        r += 1


class Ring:
    def __init__(self, items):
        self.items = items
        self.i = 0

    def next(self):
        x = self.items[self.i]
        self.i = (self.i + 1) % len(self.items)
        return x


class Builder:
    def __init__(self, S, CL, stages=None, debug=None):
        self.S, self.CL = S, CL
        self.stages = stages
        self.debug = debug or []

    def dram_in(self, name, shape, dt=F32):
        return self.nc.dram_tensor(name, list(shape), dt, kind="ExternalInput").ap()

    def dram_tmp(self, name, shape, dt):
        return self.nc.dram_tensor(name, list(shape), dt, kind="Internal").ap()

    def on(self, name):
        return self.stages is None or name in self.stages

    def build(self):
        nc = bass.Bass("TRN2", target_bir_lowering=False)
        self.nc = nc
        S, CL = self.S, self.CL
        I = {}
        I["x"] = self.dram_in("x", [S, D])
        I["ctx"] = self.dram_in("ctx", [CL, D])
        I["c_col"] = self.dram_in("c_col", [128, 8, 2])
        I["ada_w"] = self.dram_in("ada_w", [2, D, 9 * D])
        I["ada_b"] = self.dram_in("ada_b", [2, 9 * D])
        I["ada_b_col"] = self.dram_in("ada_b_col", [2, 128, 72])
        I["norm_g"] = self.dram_in("norm_g", [2, 6, D])
        I["norm_g_col"] = self.dram_in("norm_g_col", [2, 128, 6, 8])
        I["ffn_wi"] = self.dram_in("ffn_wi", [2, 2, NJ, 128, 2 * 8 * 128])
        I["ffn_wo"] = self.dram_in("ffn_wo", [2, 2, 128, NJ * D])
        I["ssd_w_in"] = self.dram_in("ssd_w_in", [D, 6208])
        I["ssd_cw"] = self.dram_in("ssd_cw", [128, 32, 5])
        I["ssd_cb"] = self.dram_in("ssd_cb", [128, 32])
        I["ssd_dt_bias"] = self.dram_in("ssd_dt_bias", [1, 64])
        I["ssd_A_log"] = self.dram_in("ssd_A_log", [1, 64])
        I["ssd_D"] = self.dram_in("ssd_D", [1, 32])
        I["ssd_ng_col"] = self.dram_in("ssd_ng_col", [128, 16])
        I["ssd_w_out"] = self.dram_in("ssd_w_out", [2048, D])
        I["gm_w_in"] = self.dram_in("gm_w_in", [D, 4096])
        I["gm_w_out"] = self.dram_in("gm_w_out", [2048, D])
        I["gm_wsT"] = self.dram_in("gm_wsT", [128, 8, 128])
        I["gm_bsT"] = self.dram_in("gm_bsT", [128, 8])
        I["gm_v_g"] = self.dram_in("gm_v_g", [1, 2048])
        I["gm_v_b"] = self.dram_in("gm_v_b", [1, 2048])
        self.I = I
        self.out = nc.dram_tensor("out", [S, D], F32, kind="ExternalOutput").ap()
        self.dbg = {n: nc.dram_tensor("dbg_" + n, list(shp), dt, kind="ExternalOutput").ap() for n, shp, dt in self.debug}
        self.R = self.dram_tmp("R", [S, D], F32)
        self.Rc = self.dram_tmp("Rc", [CL, D], F32)
        self.wi_bf = self.dram_tmp("wi_bf", [2, 2, NJ, 128, 2 * 8 * 128], BF16)
        self.wo_bf = self.dram_tmp("wo_bf", [2, 2, 128, NJ * D], BF16)
        self.ada_bf = self.dram_tmp("ada_bf", [2, D, 9 * D], BF16)
        self.ssd_win_bf = self.dram_tmp("ssd_win_bf", [D, 6208], BF16)
        self.ssd_wout_bf = self.dram_tmp("ssd_wout_bf", [2048, D], BF16)
        self.gm_win_bf = self.dram_tmp("gm_win_bf", [D, 4096], BF16)
        self.gm_wout_bf = self.dram_tmp("gm_wout_bf", [2048, D], BF16)
        self.bg = {}

        with ExitStack() as es:
            P = Prog(nc, es)
            self.P = P
            self.ps = PsumRing(P, es)
            self.consts(es)
            self.prepass()
            P.barrier()
            for l in range(2):
                self.layer(l)
            b_fin = P.buf(None, "fin")
            if "Rc" in self.dbg:
                P.dma("sp", self.dbg["Rc"][:, :], self.Rc[:, :], writes=[b_fin])
            for n in self.dbg:
                if "_" in n:
                    a, b = n.split("_")
                    src = self.scr[b][a]
                    P.dma("sp", self.dbg[n], src, writes=[b_fin])
            P.barrier()
            P.emit()
        return nc

    def consts(self, es):
        P = self.P
        idf = P.sb(es, "idf", [128, 128], F32)
        idb = P.sb(es, "idb", [128, 128], BF16)
        self.b_idf = P.buf(idf[:], "idf")
        self.b_idb = P.buf(idb[:], "idb")
        self.idf, self.idb = idf, idb
        P.op("pool", "memset", idf[:], 1.0, writes=[self.b_idf])
        P.op("pool", "affine_select", out=idf[:], in_=idf[:], pattern=[[1, 128]], compare_op=ALU.is_equal,
             fill=0.0, base=0, channel_multiplier=-1, reads=[self.b_idf], writes=[self.b_idf])
        P.op("dve", "tensor_copy", out=idb[:], in_=idf[:], reads=[self.b_idf], writes=[self.b_idb])

    def cast_ffn_weights(self, l, s):
        P, I = self.P, self.I
        if not self.on("ffn%d%d" % (l, s)):
            return
        for j0 in range(0, NJ, 2):
            P.dma("pool", self.wi_bf[l, s, j0:j0 + 2].rearrange("j p n -> (j p) n"),
                  I["ffn_wi"][l, s, j0:j0 + 2].rearrange("j p n -> (j p) n"), writes=[self.b_wcast])
        for h in range(4):
            n = NJ * D // 4
            P.dma("pool", self.wo_bf[l, s, :, h * n:(h + 1) * n], I["ffn_wo"][l, s, :, h * n:(h + 1) * n],
                  writes=[self.b_wcast])

    def bg_cast(self, name, dst, src, rows_per=256):
        P = self.P
        b = self.bgbuf[name]
        n = src.shape[0]
        for r0 in range(0, n, rows_per):
            P.dma("pool", dst[r0:r0 + rows_per, :], src[r0:r0 + rows_per, :], writes=[b], bg=True)
        self.bg[name] = (b.sem, b.cnt)

    def prepass(self):
        P = self.P
        self.b_wcast = P.buf(None, "wcast")
        self.bgbuf = {k: P.buf(None, "bg_" + k) for k in ("ada0", "ada1", "ssd", "gm")}
        self.cast_ffn_weights(0, 0)

    def prep_mod(self, es, l, s, want_ctx):
        P, I = self.P, self.I
        gs_col = P.sb(es, "gs_col", [128, 8, 2], F32)
        sh_col = P.sb(es, "sh_col", [128, 8, 2], F32)
        gg = P.sb(es, "gg", [128, 2, 1024], F32)
        with ExitStack() as es2:
            mk = P.mark()
            r = self._prep_mod(es2, l, s, want_ctx, gs_col, sh_col, gg)
            P.barrier()
            P.release(mk)
        return r

    def _prep_mod(self, es, l, s, want_ctx, gs_col, sh_col, gg):
        P, I = self.P, self.I
        cc = P.sb(es, "cc", [128, 8, 2], F32)
        sc = P.sb(es, "sc", [128, 8, 2], BF16)
        screp = P.sb(es, "screp", [128, 8, 2, 128], BF16)
        aw = [P.sb(es, "aw%d" % i, [128, 8, 1024], BF16) for i in range(2)]
        bcol = P.sb(es, "bcol", [128, 72], F32)
        gcol = P.sb(es, "gcol", [128, 6, 8], F32)
        brow = P.sb(es, "brow", [128, 1024], F32)
        grow = P.sb(es, "grow", [128, 1024], F32)
        tmpc = P.sb(es, "tmpc", [128, 8, 2], F32)
        b_cc, b_sc, b_screp = P.buf(cc, "cc"), P.buf(sc, "sc"), P.buf(screp, "screp")
        b_aw = [P.buf(aw[i], "aw%d" % i) for i in range(2)]
        b_bcol, b_gcol, b_brow, b_grow = P.buf(bcol, "bcol"), P.buf(gcol, "gcol"), P.buf(brow, "brow"), P.buf(grow, "grow")
        b_gs, b_sh, b_gg, b_tmpc = P.buf(gs_col, "gs"), P.buf(sh_col, "sh"), P.buf(gg, "gg"), P.buf(tmpc, "tmpc")
        P.dma("sp", cc[:], I["c_col"][:, :, :], writes=[b_cc])
        P.dma("sp", bcol[:], I["ada_b_col"][l], writes=[b_bcol])
        P.dma("sp", gcol[:], I["norm_g_col"][l], writes=[b_gcol])
        P.dma("sp", brow[:], I["ada_b"][l:l + 1, (3 * s + 2) * D:(3 * s + 3) * D].to_broadcast([128, D]), writes=[b_brow])
        P.dma("sp", grow[:], I["norm_g"][l, 2 * s + 1:2 * s + 2, :].to_broadcast([128, D]), writes=[b_grow])
        P.op("act", "activation", out=sc[:], in_=cc[:], func=AF.Silu, reads=[b_cc], writes=[b_sc])
        P.op("dve", "tensor_copy", out=screp[:], in_=sc[:].unsqueeze(3).to_broadcast([128, 8, 2, 128]),
             reads=[b_sc], writes=[b_screp])
        wgt = 0.5 if s != 1 else 1.0
        for r in range(3):
            m = 3 * s + r
            a, b_a = aw[r % 2], b_aw[r % 2]
            ev = self.bg.get("ada%d" % l)
            if ev is not None:
                P.dma("sp", a[:], self.ada_bf[l, :, m * D:(m + 1) * D].rearrange("(kc p) n -> p kc n", p=128), writes=[b_a], after=ev)
            else:
                P.dma("pool", a[:], I["ada_w"][l, :, m * D:(m + 1) * D].rearrange("(kc p) n -> p kc n", p=128), writes=[b_a])
            if r < 2:
                pb, pt = self.ps.one()
                pv = pt[:, 0:16].rearrange("p (c v) -> p c v", v=2)
                for dc in range(8):
                    for kc in range(8):
                        P.op("pe", "matmul", pv[:, dc, :], a[:, kc, dc * 128:(dc + 1) * 128], sc[:, kc, :],
                             start=(kc == 0), stop=(kc == 7), reads=[b_a, b_sc], writes=[pb])
                bc = bcol[:, m * 8:(m + 1) * 8].unsqueeze(2).to_broadcast([128, 8, 2])
                if r == 0:
                    P.op("dve", "tensor_tensor", out=sh_col[:], in0=pv, in1=bc, op=ALU.add,
                         reads=[pb, b_bcol], writes=[b_sh])
                else:
                    P.op("dve", "tensor_tensor", out=tmpc[:], in0=pv, in1=bc, op=ALU.add,
                         reads=[pb, b_bcol], writes=[b_tmpc])
                    gc = gcol[:, 2 * s, :].unsqueeze(2).to_broadcast([128, 8, 2])
                    P.op("dve", "scalar_tensor_tensor", out=gs_col[:], in0=tmpc[:], scalar=1.0, in1=gc,
                         op0=ALU.add, op1=ALU.mult, reads=[b_tmpc, b_gcol], writes=[b_gs])
            else:
                for v in range(2 if want_ctx else 1):
                    for h in range(2):
                        pb, pt = self.ps.one()
                        for kc in range(8):
                            P.op("pe", "matmul", pt, screp[:, kc, v, :], a[:, kc, h * 512:(h + 1) * 512],
                                 start=(kc == 0), stop=(kc == 7), reads=[b_a, b_screp], writes=[pb])
                        o = gg[:, v, h * 512:(h + 1) * 512]
                        P.op("dve", "tensor_tensor", out=o, in0=pt, in1=brow[:, h * 512:(h + 1) * 512], op=ALU.add,
                             reads=[pb, b_brow], writes=[b_gg])
                        P.op("dve", "scalar_tensor_tensor", out=o, in0=o, scalar=wgt, in1=grow[:, h * 512:(h + 1) * 512],
                             op0=ALU.mult, op1=ALU.mult, reads=[b_gg, b_grow], writes=[b_gg])
        return dict(gs=gs_col, sh=sh_col, gg=gg, b_gs=b_gs, b_sh=b_sh, b_gg=b_gg)

    def alloc_common(self, es, nsub, lite=False):
        P = self.P
        c = {}
        c["xt"] = [[P.sb(es, "xt%d_%d" % (i, s), [128, D], F32) for s in range(nsub)] for i in range(2)]
        c["b_xt"] = [[P.buf(c["xt"][i][s], "xt%d_%d" % (i, s)) for s in range(nsub)] for i in range(2)]
        c["junk"] = P.sb(es, "junk", [128, D], BF16)
        c["b_junk"] = P.buf(c["junk"], "junk")
        c["st"] = P.sb(es, "st", [128, 64], F32)
        c["b_st"] = [P.buf(c["st"][:, 4 * i:4 * i + 4], "st%d" % i) for i in range(16)]
        c["st_ring"] = Ring(list(range(16)))
        ntmp = 1 if lite else 2
        c["tmp"] = [P.sb(es, "tmp%d" % i, [128, D], F32) for i in range(ntmp)]
        c["b_tmp"] = [P.buf(c["tmp"][i], "tmp%d" % i) for i in range(ntmp)]
        c["tmp_ring"] = Ring(list(range(ntmp)))
        if lite:
            return c
        c["xs"] = [P.sb(es, "xs%d" % i, [128, D], BF16) for i in range(2)]
        c["b_xs"] = [P.buf(c["xs"][i], "xs%d" % i) for i in range(2)]
        c["xs_ring"] = Ring([0, 1])
        c["xnT"] = [P.sb(es, "xnT%d" % i, [128, 8, 128 * nsub], BF16) for i in range(2)]
        c["b_xnT"] = [[P.buf(c["xnT"][i][:, :, s * 128:(s + 1) * 128], "xnT%d_%d" % (i, s)) for s in range(nsub)] for i in range(2)]
        return c

    def load_tile(self, c, slot, src, t0, n):
        for s in range(n // 128):
            self.P.dma("sp", c["xt"][slot][s][:], src[t0 + s * 128:t0 + (s + 1) * 128, :], writes=[c["b_xt"][slot][s]])

    def store_tile(self, c, slot, dst, t0, n):
        for s in range(n // 128):
            self.P.dma("sp", dst[t0 + s * 128:t0 + (s + 1) * 128, :], c["xt"][slot][s][:], reads=[c["b_xt"][slot][s]])

    def rstd_of(self, c, src_ap, src_bufs, n_feat):
        P = self.P
        k = c["st_ring"].next()
        st, b_st = c["st"][:, 4 * k:4 * k + 4], c["b_st"][k]
        junk, b_junk = c["junk"], c["b_junk"]
        P.op("act", "activation", out=junk[:, :n_feat], in_=src_ap, func=AF.Square, accum_out=st[:, 0:1],
             reads=list(src_bufs), writes=[b_junk, b_st])
        if c.get("lnexp"):
            P.op("act", "activation", out=st[:, 1:2], in_=st[:, 0:1], func=AF.Ln, scale=1.0 / n_feat, bias=EPS, reads=[b_st], writes=[b_st])
            P.op("act", "activation", out=st[:, 2:3], in_=st[:, 1:2], func=AF.Exp, scale=-0.5, reads=[b_st], writes=[b_st])
            return st[:, 2:3], b_st
        P.op("act", "activation", out=st[:, 1:2], in_=st[:, 0:1], func=AF.Sqrt, scale=1.0 / n_feat, bias=EPS,
             reads=[b_st], writes=[b_st])
        P.op("dve", "reciprocal", out=st[:, 2:3], in_=st[:, 1:2], reads=[b_st], writes=[b_st])
        return st[:, 2:3], b_st

    def prenorm(self, c, slot, n, mod, v):
        for s in range(n // 128):
            self.prenorm_one(c, c["xt"][slot][s], c["b_xt"][slot][s], c["xnT"][slot][:, :, s * 128:(s + 1) * 128], c["b_xnT"][slot][s], mod, v)

    def prenorm_one(self, c, x, b_x, o, b_o, mod, v):
        P = self.P
        if True:
            rstd, b_st = self.rstd_of(c, x[:], [b_x], D)
            k = c["xs_ring"].next()
            xs, b_xs = c["xs"][k], c["b_xs"][k]
            P.op("dve", "tensor_scalar", out=xs[:], in0=x[:], scalar1=rstd, scalar2=None, op0=ALU.mult,
                 reads=[b_x, b_st], writes=[b_xs])
            pb, pt = self.ps.one()
            ptb = pt.bitcast(BF16).rearrange("p (c t) -> p c t", t=128)
            for kc in range(8):
                P.op("pe", "transpose", ptb[:, kc, :], xs[:, kc * 128:(kc + 1) * 128], self.idb[:],
                     reads=[b_xs, self.b_idb], writes=[pb])
            gsb = mod["gs"][:, :, v:v + 1].to_broadcast([128, 8, 128])
            shb = mod["sh"][:, :, v:v + 1].to_broadcast([128, 8, 128])
            k2 = c["tmp_ring"].next()
            tmp, b_tmp = c["tmp"][k2], c["b_tmp"][k2]
            tv = tmp[:].rearrange("p (c t) -> p c t", t=128)
            P.op("dve", "tensor_tensor", out=tv, in0=ptb, in1=gsb, op=ALU.mult, reads=[pb, mod["b_gs"]], writes=[b_tmp])
            P.op("pool", "tensor_tensor", out=o, in0=tv, in1=shb, op=ALU.add, reads=[b_tmp, mod["b_sh"]], writes=[b_o])

    def epilogue(self, c, pbs, pt2, x, b_x, mod, v):
        P = self.P
        yv = pt2.rearrange("p a b -> p (a b)")
        rstd, b_st = self.rstd_of(c, yv, pbs, D)
        k2 = c["tmp_ring"].next()
        tmp, b_tmp = c["tmp"][k2], c["b_tmp"][k2]
        P.op("dve", "tensor_tensor", out=tmp[:], in0=yv, in1=mod["gg"][:, v, :], op=ALU.mult,
             reads=list(pbs) + [mod["b_gg"]], writes=[b_tmp])
        P.op("dve", "scalar_tensor_tensor", out=x[:], in0=tmp[:], scalar=rstd, in1=x[:], op0=ALU.mult, op1=ALU.add,
             reads=[b_tmp, b_st, b_x], writes=[b_x])

    def ffn(self, l, fs, streams, nxt=None):
        P = self.P
        s_idx = 0 if fs == 0 else 2
        with ExitStack() as es:
            mk = P.mark()
            if nxt is not None:
                self.cast_ffn_weights(*nxt)
            if l == 0 and fs == 0 and BGCAST >= 2:
                I = self.I
                self.bg_cast("ada0", self.ada_bf[0], I["ada_w"][0], 128)
                self.bg_cast("ssd", self.ssd_win_bf, I["ssd_w_in"], 128)
                self.bg_cast("ssd", self.ssd_wout_bf, I["ssd_w_out"], 256)
            mod = self.prep_mod(es, l, s_idx, any(v == 1 for _, _, _, v in streams))
            c = self.alloc_common(es, 4)
            wo = P.sb(es, "wo", [128, NJ, D], BF16)
            b_wo = P.buf(wo, "wo")
            for h in range(2):
                n = NJ // 2
                P.dma("sp", wo[:, h * n:(h + 1) * n, :], self.wo_bf[l, fs, :, h * n * D:(h + 1) * n * D].rearrange("p (j d) -> p j d", d=D),
                      writes=[b_wo])
            wi = [P.sb(es, "wi%d" % i, [128, 2, 8, 128], BF16) for i in range(4)]
            b_wi = [P.buf(wi[i], "wi%d" % i) for i in range(4)]
            wring = Ring([0, 1, 2, 3])
            gT = P.sb(es, "gT", [128, NJ, 512], BF16)
            b_gT = [P.buf(gT[:, j, :], "gT%d" % j) for j in range(NJ)]
            sg = [P.sb(es, "sg%d" % i, [128, 512], F32) for i in range(2)]
            b_sg = [P.buf(sg[i], "sg%d" % i) for i in range(2)]
            sgring = Ring([0, 1])
            tiles = []
            for (R, Rd, n, v) in streams:
                for t0 in range(0, n, 512):
                    tiles.append((R, t0, min(512, n - t0), v, Rd))
            for ti, (R, t0, T, v, Rd) in enumerate(tiles):
                slot = ti % 2
                if ti == 0:
                    self.load_tile(c, slot, R, t0, T)
                    self.prenorm(c, slot, T, mod, v)
                if ti + 1 < len(tiles):
                    R2, t2, T2, v2, _ = tiles[ti + 1]
                    self.load_tile(c, 1 - slot, R2, t2, T2)
                xnT = c["xnT"][slot]
                b_xn = c["b_xnT"][slot][:T // 128]
                for j in range(NJ):
                    k = wring.next()
                    w, b_w = wi[k], b_wi[k]
                    P.dma("sp", w[:].rearrange("p a k f -> p (a k f)"), self.wi_bf[l, fs, j], writes=[b_w])
                    pbg, ptg = self.ps.one()
                    pbu, ptu = self.ps.one()
                    for (pb, pt, gu) in ((pbg, ptg, 0), (pbu, ptu, 1)):
                        for kc in range(8):
                            P.op("pe", "matmul", pt[:, :T], w[:, gu, kc, :], xnT[:, kc, :T], start=(kc == 0), stop=(kc == 7),
                                 reads=[b_w] + b_xn, writes=[pb])
                    k2 = sgring.next()
                    P.op("act", "activation", out=sg[k2][:, :T], in_=ptg[:, :T], func=AF.Silu, reads=[pbg], writes=[b_sg[k2]])
                    P.op("dve", "tensor_tensor", out=gT[:, j, :T], in0=sg[k2][:, :T], in1=ptu[:, :T], op=ALU.mult,
                         reads=[b_sg[k2], pbu], writes=[b_gT[j]])
                    if j == NJ // 2 and ti + 1 < len(tiles):
                        self.prenorm(c, 1 - slot, tiles[ti + 1][2], mod, tiles[ti + 1][3])
                for s in range(T // 128):
                    pbs, pt2 = self.ps.two()
                    for h in range(2):
                        for j in range(NJ):
                            P.op("pe", "matmul", pt2[:, h, :], gT[:, j, s * 128:(s + 1) * 128], wo[:, j, h * 512:(h + 1) * 512],
                                 start=(j == 0), stop=(j == NJ - 1), reads=[b_gT[j], b_wo], writes=[pbs[h]])
                    self.epilogue(c, pbs, pt2, c["xt"][slot][s], c["b_xt"][slot][s], mod, v)
                self.store_tile(c, slot, Rd, t0, T)
            P.barrier()
            P.release(mk)

    def gmlp(self, l, R, n):
        P, I = self.P, self.I
        with ExitStack() as es:
            mk = P.mark()
            mod = self.prep_mod(es, l, 1, False)
            c = self.alloc_common(es, 1)
            xt3 = P.sb(es, "xt3", [128, D], F32)
            xts = [c["xt"][0][0], c["xt"][1][0], xt3]
            b_xts = [c["b_xt"][0][0], c["b_xt"][1][0], P.buf(xt3, "xt3")]
            wg = P.sb(es, "wg", [128, 8, 4096], BF16)
            b_wg = P.buf(wg, "wg")
            ev = self.bg.get("gm")
            for kc in range(8):
                if ev is not None:
                    P.dma("sp", wg[:, kc, :], self.gm_win_bf[kc * 128:(kc + 1) * 128, :], writes=[b_wg], after=ev)
                else:
                    P.dma("pool", wg[:, kc, :], I["gm_w_in"][kc * 128:(kc + 1) * 128, :], writes=[b_wg])
            wo2 = P.sb(es, "wo2", [128, 16, D], BF16)
            b_wo2 = P.buf(wo2, "wo2")
            for q in range(4):
                if ev is not None:
                    P.dma("sp", wo2[:, 4 * q:4 * q + 4, :], self.gm_wout_bf[q * 512:(q + 1) * 512, :].rearrange("(j p) d -> p j d", p=128),
                          writes=[b_wo2], after=ev)
                else:
                    P.dma("pool", wo2[:, 4 * q:4 * q + 4, :], I["gm_w_out"][q * 512:(q + 1) * 512, :].rearrange("(j p) d -> p j d", p=128),
                          writes=[b_wo2])
            wsT = P.sb(es, "wsT", [128, 8, 128], BF16)
            b_wsT = P.buf(wsT, "wsT")
            P.dma("pool", wsT[:], I["gm_wsT"][:, :, :], writes=[b_wsT])
            bsT = P.sb(es, "bsT", [128, 8], F32)
            b_bsT = P.buf(bsT, "bsT")
            P.dma("sp", bsT[:], I["gm_bsT"][:, :], writes=[b_bsT])
            vg = P.sb(es, "vg", [128, 2048], F32)
            vb = P.sb(es, "vb", [128, 2048], F32)
            b_vg, b_vb = P.buf(vg, "vg"), P.buf(vb, "vb")
            P.dma("sp", vg[:], I["gm_v_g"][0:1, :].to_broadcast([128, 2048]), writes=[b_vg])
            P.dma("sp", vb[:], I["gm_v_b"][0:1, :].to_broadcast([128, 2048]), writes=[b_vb])
            junk2 = P.sb(es, "junk2", [128, 2048], BF16)
            b_junk2 = P.buf(junk2, "junk2")
            guf, gvf, gvn, mm, mT, b_guf, b_gvf, b_gvn, b_mm, b_mT = [], [], [], [], [], [], [], [], [], []
            for i in range(2):
                guf.append(P.sb(es, "guf%d" % i, [128, 2048], BF16))
                gvf.append(P.sb(es, "gvf%d" % i, [128, 2048], F32))
                gvn.append(P.sb(es, "gvn%d" % i, [128, 2048], BF16))
                mm.append(P.sb(es, "mm%d" % i, [128, 2048], BF16))
                mT.append(P.sb(es, "mT%d" % i, [128, 16, 128], BF16))
                b_guf.append([P.buf(guf[i][:, q * 512:(q + 1) * 512], "guf%d_%d" % (i, q)) for q in range(4)])
                b_gvf.append([P.buf(gvf[i][:, q * 512:(q + 1) * 512], "gvf%d_%d" % (i, q)) for q in range(4)])
                b_gvn.append(P.buf(gvn[i], "gvn%d" % i))
                b_mm.append(P.buf(mm[i], "mm%d" % i))
                b_mT.append([P.buf(mT[i][:, 8 * q:8 * q + 8, :], "mT%d_%d" % (i, q)) for q in range(2)])
            lnst = P.sb(es, "lnst", [128, 16], F32)
            b_ln = [P.buf(lnst[:, 8 * i:8 * i + 8], "ln%d" % i) for i in range(2)]
            nt = n // 128

            def chunk(ti):
                slot = ti % 2
                x3 = ti % 3
                t0 = ti * 128
                xt, b_xt = xts[x3], b_xts[x3]
                xnT, b_xn = c["xnT"][slot], c["b_xnT"][slot][0]
                ln, b_l = lnst[:, 8 * slot:8 * slot + 8], b_ln[slot]
                G, V, N_, Mm, MT = guf[slot], gvf[slot], gvn[slot], mm[slot], mT[slot]
                bG, bV, bN, bM, bMT = b_guf[slot], b_gvf[slot], b_gvn[slot], b_mm[slot], b_mT[slot]
                P.dma("sp", xt[:], R[t0:t0 + 128, :], writes=[b_xt])
                self.prenorm_one(c, xt, b_xt, c["xnT"][slot][:, :, 0:128], b_xn, mod, 0)
                yield
                for cb in range(8):
                    pb, pt = self.ps.one()
                    for kc in range(8):
                        P.op("pe", "matmul", pt, xnT[:, kc, :], wg[:, kc, cb * 512:(cb + 1) * 512], start=(kc == 0), stop=(kc == 7),
                             reads=[b_xn, b_wg], writes=[pb])
                    if cb < 4:
                        P.op("act", "activation", out=G[:, cb * 512:(cb + 1) * 512], in_=pt, func=AF.Gelu_apprx_tanh,
                             reads=[pb], writes=[bG[cb]])
                    else:
                        k = cb - 4
                        P.op("act", "activation", out=V[:, k * 512:(k + 1) * 512], in_=pt, func=AF.Gelu_apprx_tanh,
                             accum_out=ln[:, k:k + 1], reads=[pb], writes=[bV[k], b_l])
                yield
                P.op("act", "activation", out=junk2[:], in_=V[:], func=AF.Square, accum_out=ln[:, 4:5],
                     reads=bV, writes=[b_junk2, b_l])
                P.op("dve", "tensor_reduce", out=ln[:, 5:6], in_=ln[:, 0:4], axis=AX.X, op=ALU.add, reads=[b_l], writes=[b_l])
                P.op("dve", "tensor_scalar", out=ln[:, 5:6], in0=ln[:, 5:6], scalar1=1.0 / 2048, scalar2=None, op0=ALU.mult,
                     reads=[b_l], writes=[b_l])
                P.op("dve", "tensor_tensor", out=ln[:, 6:7], in0=ln[:, 5:6], in1=ln[:, 5:6], op=ALU.mult, reads=[b_l], writes=[b_l])
                P.op("dve", "scalar_tensor_tensor", out=ln[:, 6:7], in0=ln[:, 4:5], scalar=1.0 / 2048, in1=ln[:, 6:7],
                     op0=ALU.mult, op1=ALU.subtract, reads=[b_l], writes=[b_l])
                P.op("act", "activation", out=ln[:, 7:8], in_=ln[:, 6:7], func=AF.Sqrt, bias=EPS, reads=[b_l], writes=[b_l])
                P.op("dve", "reciprocal", out=ln[:, 7:8], in_=ln[:, 7:8], reads=[b_l], writes=[b_l])
                for hh in range(2):
                    vs = slice(hh * 1024, (hh + 1) * 1024)
                    bv2 = bV[2 * hh:2 * hh + 2]
                    P.op("dve", "tensor_scalar", out=V[:, vs], in0=V[:, vs], scalar1=ln[:, 5:6], scalar2=ln[:, 7:8], op0=ALU.subtract, op1=ALU.mult,
                         reads=bv2 + [b_l], writes=bv2)
                    P.op("pool" if (hh == 0 and not GV & 2) else "dve", "tensor_tensor", out=V[:, vs], in0=V[:, vs], in1=vg[:, vs], op=ALU.mult, reads=bv2 + [b_vg], writes=bv2)
                    P.op("dve" if (GV & 4 and hh == 1) else "pool", "tensor_tensor", out=N_[:, vs], in0=V[:, vs], in1=vb[:, vs], op=ALU.add, reads=bv2 + [b_vb], writes=[bN])
                yield
                for g2 in range(4):
                    pb, pt = self.ps.one()
                    for gi in range(2):
                        g = 2 * g2 + gi
                        P.op("pe", "matmul", pt[:, gi * 256:(gi + 1) * 256], wsT[:, g, :], N_[:, g * 256:(g + 1) * 256], start=True, stop=True,
                             reads=[b_wsT, bN], writes=[pb])
                    for gi in range(2):
                        g = 2 * g2 + gi
                        P.op("dve", "scalar_tensor_tensor", out=Mm[:, g * 256:(g + 1) * 256], in0=pt[:, gi * 256:(gi + 1) * 256],
                             scalar=bsT[:, g:g + 1], in1=G[:, g * 256:(g + 1) * 256], op0=ALU.add, op1=ALU.mult,
                             reads=[pb, b_bsT, bG[g // 2]], writes=[bM])
                for hh in range(2):
                    pb, pt = self.ps.one()
                    ptb = pt.bitcast(BF16).rearrange("p (c t) -> p c t", t=128)
                    for q in range(8):
                        j = hh * 8 + q
                        P.op("pe", "transpose", ptb[:, q, :], Mm[:, j * 128:(j + 1) * 128], self.idb[:], reads=[bM, self.b_idb], writes=[pb])
                    P.op("act", "activation", out=MT[:, hh * 8:(hh + 1) * 8, :], in_=ptb, func=AF.Copy, reads=[pb], writes=[bMT[hh]])
                yield
                pbs, pt2 = self.ps.two()
                for h in range(2):
                    for j in range(16):
                        P.op("pe", "matmul", pt2[:, h, :], MT[:, j, :], wo2[:, j, h * 512:(h + 1) * 512], start=(j == 0), stop=(j == 15),
                             reads=[bMT[j // 8], b_wo2], writes=[pbs[h]])
                self.epilogue(c, pbs, pt2, xt, b_xt, mod, 0)
                P.dma("sp", R[t0:t0 + 128, :], xt[:], reads=[b_xt])

            run_pipeline([chunk(ti) for ti in range(nt)], 3 if GV & 1 else 2)
            P.barrier()
            P.release(mk)

    def ssd_consts(self, es):
        P, I = self.P, self.I
        k = {}
        for nm in ("Lf", "Lb", "ones"):
            k[nm] = P.sb(es, nm, [128, 128], F32)
            k["b_" + nm] = P.buf(k[nm], nm)
            P.op("pool", "memset", k[nm][:], 1.0, writes=[k["b_" + nm]])
        P.op("pool", "affine_select", out=k["Lf"][:], in_=k["Lf"][:], pattern=[[1, 128]], compare_op=ALU.is_ge, fill=0.0, base=0,
             channel_multiplier=-1, reads=[k["b_Lf"]], writes=[k["b_Lf"]])
        P.op("pool", "affine_select", out=k["Lb"][:], in_=k["Lb"][:], pattern=[[-1, 128]], compare_op=ALU.is_ge, fill=0.0, base=0,
             channel_multiplier=1, reads=[k["b_Lb"]], writes=[k["b_Lb"]])
        ntmp = P.sb(es, "ntmp", [128, 4, 128], F32)
        b_ntmp = P.buf(ntmp, "ntmp")
        for nm, pat, cm in (("negF", [[0, 4], [-1, 128]], 1), ("negB", [[0, 4], [1, 128]], -1)):
            k[nm] = P.sb(es, nm, [128, 4, 128], BF16)
            k["b_" + nm] = P.buf(k[nm], nm)
            P.op("pool", "memset", ntmp[:], -30000.0, writes=[b_ntmp])
            P.op("pool", "affine_select", out=ntmp[:], in_=ntmp[:], pattern=pat, compare_op=ALU.is_gt, fill=0.0, base=0,
                 channel_multiplier=cm, reads=[b_ntmp], writes=[b_ntmp])
            P.op("dve", "tensor_copy", out=k[nm][:], in_=ntmp[:], reads=[b_ntmp], writes=[k["b_" + nm]])
        k["Abc"] = P.sb(es, "Abc", [128, 64], F32)
        k["dtb"] = P.sb(es, "dtb", [128, 64], F32)
        k["Dbc"] = P.sb(es, "Dbc", [128, 32], F32)
        for nm in ("Abc", "dtb", "Dbc"):
            k["b_" + nm] = P.buf(k[nm], nm)
        P.dma("sp", k["Abc"][:], I["ssd_A_log"][0:1, :].to_broadcast([128, 64]), writes=[k["b_Abc"]])
        P.dma("sp", k["dtb"][:], I["ssd_dt_bias"][0:1, :].to_broadcast([128, 64]), writes=[k["b_dtb"]])
        P.dma("sp", k["Dbc"][:], I["ssd_D"][0:1, :].to_broadcast([128, 32]), writes=[k["b_Dbc"]])
        P.op("act", "activation", out=k["Abc"][:], in_=k["Abc"][:], func=AF.Exp, reads=[k["b_Abc"]], writes=[k["b_Abc"]])
        P.op("dve", "tensor_scalar", out=k["Abc"][:], in0=k["Abc"][:], scalar1=-1.0, scalar2=None, op0=ALU.mult,
             reads=[k["b_Abc"]], writes=[k["b_Abc"]])
        return k

    def ssd(self, l):
        P, I = self.P, self.I
        S, CL = self.S, self.CL
        seqs = [("c", self.Rc, CL, 1), ("x", self.R, S, 0)]
        scr = {}
        for nm, _, L, _ in seqs:
            d = {}
            d["raw"] = self.dram_tmp("raw_" + nm, [128, 32, L + 4], BF16)
            d["zs"] = self.dram_tmp("zs_" + nm, [L, 2048], BF16)
            d["dt"] = self.dram_tmp("dt_" + nm, [L, 64], F32)
            d["xs"] = self.dram_tmp("xs_" + nm, [L, 2048], BF16)
            d["Bt"] = self.dram_tmp("Bt_" + nm, [L, 1024], BF16)
            d["BT"] = self.dram_tmp("BT_" + nm, [128, 8, L], BF16)
            d["CT"] = self.dram_tmp("CT_" + nm, [128, 8, L], BF16)
            d["hb"] = self.dram_tmp("hb_" + nm, [L // 128, 128, 2048], BF16)
            scr[nm] = d
        self.scr = scr
        with ExitStack() as es:
            mk0 = P.mark()
            mod = self.prep_mod(es, l, 1, True)
            K = self.ssd_consts(es)
            if self.on("ssd_s0"):
                self.ssd_s0(mod, K, seqs, scr)
            if self.on("ssd_s1"):
                self.ssd_s1(mod, K, seqs, scr)
            if self.on("ssd_s2"):
                self.ssd_s2(mod, K, seqs, scr)
            P.barrier()
            P.release(mk0)

    def ssd_s0(self, mod, K, seqs, scr):
        P, I = self.P, self.I
        with ExitStack() as es:
            mk = P.mark()
            c = self.alloc_common(es, 4)
            wx = P.sb(es, "wx", [128, 8, 4096], BF16)
            wz = P.sb(es, "wz", [128, 8, 2048], BF16)
            wd = P.sb(es, "wd", [128, 8, 64], BF16)
            b_wx, b_wz, b_wd = P.buf(wx, "wx"), P.buf(wz, "wz"), P.buf(wd, "wd")
            ev = self.bg.get("ssd")
            for kc in range(8):
                if ev is not None:
                    rows = self.ssd_win_bf[kc * 128:(kc + 1) * 128, :]
                    P.dma("sp", wx[:, kc, :], rows[:, 2048:6144], writes=[b_wx], after=ev)
                    P.dma("sp", wz[:, kc, :], rows[:, 0:2048], writes=[b_wz], after=ev)
                    P.dma("sp", wd[:, kc, :], rows[:, 6144:6208], writes=[b_wd], after=ev)
                else:
                    rows = I["ssd_w_in"][kc * 128:(kc + 1) * 128, :]
                    P.dma("pool", wx[:, kc, :], rows[:, 2048:6144], writes=[b_wx])
                    P.dma("pool", wz[:, kc, :], rows[:, 0:2048], writes=[b_wz])
                    P.dma("pool", wd[:, kc, :], rows[:, 6144:6208], writes=[b_wd])
            zero = P.sb(es, "zero", [128, 32, 2], BF16)
            b_zero = P.buf(zero, "zero")
            P.op("pool", "memset", zero[:], 0.0, writes=[b_zero])
            stg = [P.sb(es, "stg%d" % i, [128, 4, 512], BF16) for i in range(2)]
            b_stg = [P.buf(stg[i], "stg%d" % i) for i in range(2)]
            stg_ring = Ring([0, 1])
            zst = [P.sb(es, "zst%d" % i, [128, 2048], BF16) for i in range(2)]
            b_zst = [P.buf(zst[i], "zst%d" % i) for i in range(2)]
            zst_ring = Ring([0, 1])
            dst = [P.sb(es, "dst%d" % i, [128, 64], F32) for i in range(2)]
            b_dst = [P.buf(dst[i], "dst%d" % i) for i in range(2)]
            dst_ring = Ring([0, 1])
            tiles = []
            for (nm, R, L, v) in seqs:
                d = scr[nm]
                P.dma("sp", d["raw"][:, :, 0:2], zero[:], reads=[b_zero])
                P.dma("sp", d["raw"][:, :, L + 2:L + 4], zero[:], reads=[b_zero])
                for t0 in range(0, L, 512):
                    tiles.append((nm, R, t0, min(512, L - t0), v))
            for ti, (nm, R, t0, T, v) in enumerate(tiles):
                d = scr[nm]
                slot = ti % 2
                if ti == 0:
                    self.load_tile(c, slot, R, t0, T)
                    self.prenorm(c, slot, T, mod, v)
                if ti + 1 < len(tiles):
                    nm2, R2, t2, T2, v2 = tiles[ti + 1]
                    self.load_tile(c, 1 - slot, R2, t2, T2)
                xnT = c["xnT"][slot]
                b_xn = c["b_xnT"][slot][:T // 128]
                for q in range(8):
                    k = stg_ring.next()
                    for u in range(4):
                        cc = 4 * q + u
                        pb, pt = self.ps.one()
                        for kc in range(8):
                            P.op("pe", "matmul", pt[:, :T], wx[:, kc, cc * 128:(cc + 1) * 128], xnT[:, kc, :T], start=(kc == 0), stop=(kc == 7),
                                 reads=[b_wx] + b_xn, writes=[pb])
                        if u % 2 == 0:
                            P.op("act", "activation", out=stg[k][:, u, :T], in_=pt[:, :T], func=AF.Copy, reads=[pb], writes=[b_stg[k]])
                        else:
                            P.op("dve", "tensor_copy", out=stg[k][:, u, :T], in_=pt[:, :T], reads=[pb], writes=[b_stg[k]])
                    P.dma("sp", d["raw"][:, 4 * q:4 * q + 4, 2 + t0:2 + t0 + T], stg[k][:, :, :T], reads=[b_stg[k]])
                    if q == 3 and ti + 1 < len(tiles):
                        self.prenorm(c, 1 - slot, tiles[ti + 1][3], mod, tiles[ti + 1][4])
                for s in range(T // 128):
                    k = zst_ring.next()
                    for cb in range(4):
                        pb, pt = self.ps.one()
                        for kc in range(8):
                            P.op("pe", "matmul", pt, xnT[:, kc, s * 128:(s + 1) * 128], wz[:, kc, cb * 512:(cb + 1) * 512], start=(kc == 0), stop=(kc == 7),
                                 reads=[b_wz, b_xn[s]], writes=[pb])
                        P.op("act", "activation", out=zst[k][:, cb * 512:(cb + 1) * 512], in_=pt, func=AF.Silu, reads=[pb], writes=[b_zst[k]])
                    P.dma("sp", d["zs"][t0 + s * 128:t0 + (s + 1) * 128, :], zst[k][:], reads=[b_zst[k]])
                    k = dst_ring.next()
                    pb, pt = self.ps.one()
                    for kc in range(8):
                        P.op("pe", "matmul", pt[:, 0:64], xnT[:, kc, s * 128:(s + 1) * 128], wd[:, kc, :], start=(kc == 0), stop=(kc == 7),
                             reads=[b_wd, b_xn[s]], writes=[pb])
                    P.op("dve", "tensor_tensor", out=dst[k][:], in0=pt[:, 0:64], in1=K["dtb"][:], op=ALU.add, reads=[pb, K["b_dtb"]], writes=[b_dst[k]])
                    P.op("act", "activation", out=dst[k][:], in_=dst[k][:], func=AF.Exp, reads=[b_dst[k]], writes=[b_dst[k]])
                    P.op("act", "activation", out=dst[k][:], in_=dst[k][:], func=AF.Ln, bias=1.0, reads=[b_dst[k]], writes=[b_dst[k]])
                    P.dma("sp", d["dt"][t0 + s * 128:t0 + (s + 1) * 128, :], dst[k][:], reads=[b_dst[k]])
            P.barrier()
            P.release(mk)

    def ssd_s1(self, mod, K, seqs, scr):
        P, I = self.P, self.I
        with ExitStack() as es:
            mk = P.mark()
            cw = P.sb(es, "cw", [128, 32, 5], F32)
            cb = P.sb(es, "cb", [128, 32], F32)
            b_cw, b_cb = P.buf(cw, "cw"), P.buf(cb, "cb")
            P.dma("sp", cw[:], I["ssd_cw"][:, :, :], writes=[b_cw])
            P.dma("sp", cb[:], I["ssd_cb"][:, :], writes=[b_cb])
            dg = P.sb(es, "dg", [128, 32, 5, 128], BF16)
            b_dg = [P.buf(dg[:, cc], "dg%d" % cc) for cc in range(32)]
            n_ = 0
            for cc in range(32):
                for tap in range(5):
                    P.op("dve" if n_ % 2 == 0 else "pool", "tensor_scalar", out=dg[:, cc, tap, :], in0=self.idf[:], scalar1=cw[:, cc, tap:tap + 1],
                         scalar2=None, op0=ALU.mult, reads=[self.b_idf, b_cw], writes=[b_dg[cc]])
                    n_ += 1
            W = P.sb(es, "W", [128, 32, 516], BF16)
            b_W = [P.buf(W[:, 8 * q:8 * q + 8, :], "W%d" % q) for q in range(4)]
            xcT2 = [P.sb(es, "xcT%d" % i, [128, 32, 512], BF16) for i in range(2)]
            b_xcT2 = [[P.buf(xcT2[i][:, 8 * q:8 * q + 8, :], "xcT%d_%d" % (i, q)) for q in range(4)] for i in range(2)]
            xst = [P.sb(es, "xst%d" % i, [128, 3072], BF16) for i in range(2)]
            b_xst = [[P.buf(xst[i][:, q * 1024:(q + 1) * 1024], "xst%d_%d" % (i, q)) for q in range(3)] for i in range(2)]
            dtt = [P.sb(es, "dtt%d" % i, [128, 64], F32) for i in range(2)]
            b_dtt = [P.buf(dtt[i], "dtt%d" % i) for i in range(2)]
            sm = [P.sb(es, "sm%d" % i, [128, 256], F32) for i in range(2)]
            b_sm = [P.buf(sm[i], "sm%d" % i) for i in range(2)]
            xdd = [P.sb(es, "xdd%d" % i, [128, 2048], BF16) for i in range(2)]
            b_xdd = [P.buf(xdd[i], "xdd%d" % i) for i in range(2)]
            hb = P.sb(es, "hb", [128, 2048], F32)
            b_hb = [P.buf(hb[:, q * 512:(q + 1) * 512], "hb%d" % q) for q in range(4)]
            hbb = [P.sb(es, "hbb%d" % i, [128, 2048], BF16) for i in range(2)]
            b_hbb = [P.buf(hbb[i], "hbb%d" % i) for i in range(2)]
            ht = [P.sb(es, "ht%d" % i, [128, 512], F32) for i in range(2)]
            b_ht = [P.buf(ht[i], "ht%d" % i) for i in range(2)]
            P.op("pool", "memset", hb[:], 0.0, writes=b_hb)
            P.op("pool", "memset", hbb[0][:], 0.0, writes=[b_hbb[0]])
            tiles = []
            for (nm, R, L, v) in seqs:
                T = min(512, L)
                for ti in range(L // T - 1, -1, -1):
                    tiles.append((scr[nm], ti * T, T))

            def conv_load(k):
                d, tt0, T = tiles[k]
                for q in range(4):
                    P.dma("sp", W[:, 8 * q:8 * q + 8, :T + 4], d["raw"][:, 8 * q:8 * q + 8, tt0:tt0 + T + 4], writes=[b_W[q]])

            def conv_piece(k, q):
                d, tt0, T = tiles[k]
                xcT, b_xcT = xcT2[k % 2], b_xcT2[k % 2]
                for cc in range(8 * q, 8 * q + 8):
                    pb, pt = self.ps.one()
                    for tap in range(5):
                        P.op("pe", "matmul", pt[:, :T], dg[:, cc, tap, :], W[:, cc, tap:tap + T], start=(tap == 0), stop=(tap == 4),
                             reads=[b_dg[cc], b_W[cc // 8]], writes=[pb])
                    P.op("act", "activation", out=xcT[:, cc, :T], in_=pt[:, :T], func=AF.Silu, bias=cb[:, cc:cc + 1],
                         reads=[pb, b_cb], writes=[b_xcT[cc // 8]])
                if q == 2:
                    P.dma("sp", d["BT"][:, :, tt0:tt0 + T], xcT[:, 16:24, :T], reads=[b_xcT[2]])
                if q == 3:
                    P.dma("sp", d["CT"][:, :, tt0:tt0 + T], xcT[:, 24:32, :T], reads=[b_xcT[3]])

            conv_load(0)
            for q in range(4):
                conv_piece(0, q)
            chunks = []
            for k, (d, tt0, T) in enumerate(tiles):
                for sc_ in range(T // 128 - 1, -1, -1):
                    chunks.append((k, d, tt0, T, sc_))

            def segA(n_):
                k, d, tt0, T, sc_ = chunks[n_]
                xcT, b_xcT = xcT2[k % 2], b_xcT2[k % 2]
                sl = n_ % 2
                t0 = tt0 + sc_ * 128
                P.dma("sp", dtt[sl][:], d["dt"][t0:t0 + 128, :], writes=[b_dtt[sl]])
                for q in range(3):
                    pb, pt = self.ps.one()
                    ptb = pt.bitcast(BF16).rearrange("p (c t) -> p c t", t=128)
                    for u in range(8):
                        P.op("pe", "transpose", ptb[:, u, :], xcT[:, q * 8 + u, sc_ * 128:(sc_ + 1) * 128], self.idb[:],
                             reads=[b_xcT[q], self.b_idb], writes=[pb])
                    if q == 1:
                        P.op("dve", "tensor_copy", out=xst[sl][:, q * 1024:(q + 1) * 1024], in_=pt.bitcast(BF16), reads=[pb], writes=[b_xst[sl][q]])
                    else:
                        P.op("act", "activation", out=xst[sl][:, q * 1024:(q + 1) * 1024], in_=pt.bitcast(BF16), func=AF.Copy,
                             reads=[pb], writes=[b_xst[sl][q]])
                P.dma("sp", d["xs"][t0:t0 + 128, :], xst[sl][:, 0:2048], reads=b_xst[sl][0:2])
                P.dma("sp", d["Bt"][t0:t0 + 128, :], xst[sl][:, 2048:3072], reads=[b_xst[sl][2]])
                s_ = sm[sl]
                b_s = b_sm[sl]
                P.op("dve", "tensor_tensor", out=s_[:, 0:32], in0=dtt[sl][:, 32:64], in1=K["Abc"][:, 32:64], op=ALU.mult,
                     reads=[b_dtt[sl], K["b_Abc"]], writes=[b_s])
                pb, pt = self.ps.one()
                P.op("pe", "matmul", pt[:, 0:32], K["Lb"][:], s_[:, 0:32], start=True, stop=True, reads=[K["b_Lb"], b_s], writes=[pb])
                P.op("pe", "matmul", pt[:, 32:64], K["ones"][:], s_[:, 0:32], start=True, stop=True, reads=[K["b_ones"], b_s], writes=[pb])
                P.op("act", "activation", out=s_[:, 32:96], in_=pt[:, 0:64], func=AF.Copy, reads=[pb], writes=[b_s])
                P.op("dve", "tensor_tensor", out=s_[:, 96:128], in0=s_[:, 64:96], in1=s_[:, 32:64], op=ALU.subtract, reads=[b_s], writes=[b_s])
                P.op("act", "activation", out=s_[:, 96:128], in_=s_[:, 96:128], func=AF.Exp, reads=[b_s], writes=[b_s])
                P.op("dve", "tensor_tensor", out=s_[:, 96:128], in0=s_[:, 96:128], in1=dtt[sl][:, 32:64], op=ALU.mult, reads=[b_s, b_dtt[sl]], writes=[b_s])
                P.op("act", "activation", out=s_[:, 128:160], in_=s_[:, 64:96], func=AF.Exp, reads=[b_s], writes=[b_s])
                P.op("dve", "tensor_tensor", out=xdd[sl][:].rearrange("p (h e) -> p h e", e=64), in0=xst[sl][:, 0:2048].rearrange("p (h e) -> p h e", e=64),
                     in1=s_[:, 96:128].unsqueeze(2).to_broadcast([128, 32, 64]), op=ALU.mult, reads=b_xst[sl][0:2] + [b_s], writes=[b_xdd[sl]])

            def segB(n_):
                k, d, tt0, T, sc_ = chunks[n_]
                sl = n_ % 2
                ci = tt0 // 128 + sc_
                s_, b_s = sm[sl], b_sm[sl]
                cur, hsl = n_ % 2, (n_ + 1) % 2
                P.dma("sp", d["hb"][ci], hbb[cur][:], reads=[b_hbb[cur]])
                for q in range(4):
                    pb, pt = self.ps.one()
                    for gi in range(2):
                        g = 2 * q + gi
                        P.op("pe", "matmul", pt[:, gi * 256:(gi + 1) * 256], xst[sl][:, 2048 + g * 128:2048 + (g + 1) * 128], xdd[sl][:, g * 256:(g + 1) * 256],
                             start=True, stop=True, reads=[b_xst[sl][2], b_xdd[sl]], writes=[pb])
                    hv = hb[:, q * 512:(q + 1) * 512]
                    P.op("dve", "tensor_tensor", out=ht[q % 2][:].rearrange("p (h e) -> p h e", e=64), in0=hv.rearrange("p (h e) -> p h e", e=64),
                         in1=s_[:, 128 + 8 * q:128 + 8 * q + 8].unsqueeze(2).to_broadcast([128, 8, 64]), op=ALU.mult,
                         reads=[b_hb[q], b_s], writes=[b_ht[q % 2]])
                    P.op("dve", "tensor_tensor", out=hv, in0=ht[q % 2][:], in1=pt, op=ALU.add, reads=[b_ht[q % 2], pb], writes=[b_hb[q]])
                P.op("act", "activation", out=hbb[hsl][:], in_=hb[:], func=AF.Copy, reads=b_hb, writes=[b_hbb[hsl]])

            pend = {}
            segA(0)
            for n_, (k, d, tt0, T, sc_) in enumerate(chunks):
                first_of_tile = (n_ == 0) or (chunks[n_ - 1][0] != k)
                if first_of_tile and k + 1 < len(tiles):
                    conv_load(k + 1)
                    pend[k + 1] = [0, 1, 2, 3]
                left_in_tile = sc_ + 1
                if pend.get(k + 1):
                    npc = (len(pend[k + 1]) + left_in_tile - 1) // left_in_tile
                    for _ in range(npc):
                        conv_piece(k + 1, pend[k + 1].pop(0))
                if n_ + 1 < len(chunks):
                    segA(n_ + 1)
                segB(n_)
            P.barrier()
            P.release(mk)

    def ssd_s2(self, mod, K, seqs, scr):
        P, I = self.P, self.I
        with ExitStack() as es:
            mk = P.mark()
            c = self.alloc_common(es, 1, lite=True)
            c["lnexp"] = LNEXP
            wo3 = P.sb(es, "wo3", [128, 16, D], BF16)
            b_wo3 = P.buf(wo3, "wo3")
            ev = self.bg.get("ssd")
            for q in range(4):
                if ev is not None:
                    P.dma("sp", wo3[:, 4 * q:4 * q + 4, :], self.ssd_wout_bf[q * 512:(q + 1) * 512, :].rearrange("(j p) d -> p j d", p=128),
                          writes=[b_wo3], after=ev)
                else:
                    P.dma("pool", wo3[:, 4 * q:4 * q + 4, :], I["ssd_w_out"][q * 512:(q + 1) * 512, :].rearrange("(j p) d -> p j d", p=128), writes=[b_wo3])
            ngc = P.sb(es, "ngc", [128, 16], F32)
            b_ngc = P.buf(ngc, "ngc")
            P.dma("sp", ngc[:], I["ssd_ng_col"][:, :], writes=[b_ngc])
            for j in range(16):
                P.op("dve" if j % 2 else "pool", "tensor_scalar", out=wo3[:, j, :], in0=wo3[:, j, :], scalar1=ngc[:, j:j + 1], scalar2=None, op0=ALU.mult,
                     reads=[b_wo3, b_ngc], writes=[b_wo3])
            sel = P.sb(es, "sel", [128, 64, 128], BF16)
            b_sel = P.buf(sel, "sel")
            Did = P.sb(es, "Did", [128, 32, 128], BF16)
            b_Did = P.buf(Did, "Did")
            with ExitStack() as es2:
                t1 = P.sb(es2, "selt1", [128, 16, 128], F32)
                t2 = P.sb(es2, "selt2", [128, 16, 128], F32)
                b_t1, b_t2 = P.buf(t1, "selt1"), P.buf(t2, "selt2")
                for q in range(4):
                    for (t, b_t, base) in ((t1, b_t1, -16 * q), (t2, b_t2, -16 * q - 64)):
                        P.op("pool", "memset", t[:], 1.0, writes=[b_t])
                        P.op("pool", "affine_select", out=t[:], in_=t[:], pattern=[[-1, 16], [0, 128]], compare_op=ALU.is_equal, fill=0.0,
                             base=base, channel_multiplier=1, reads=[b_t], writes=[b_t])
                    P.op("dve", "tensor_tensor", out=sel[:, 16 * q:16 * q + 16, :], in0=t1[:], in1=t2[:], op=ALU.add, reads=[b_t1, b_t2], writes=[b_sel])
                for h in range(32):
                    P.op("dve", "tensor_scalar", out=Did[:, h, :], in0=self.idf[:], scalar1=K["Dbc"][:, h:h + 1], scalar2=None, op0=ALU.mult,
                         reads=[self.b_idf, K["b_Dbc"]], writes=[b_Did])
                P.barrier()
            dF = P.sb(es, "dF", [128, 128], F32)
            dB = P.sb(es, "dB", [128, 128], F32)
            b_dF, b_dB = P.buf(dF, "dF"), P.buf(dB, "dB")
            P.op("pool", "memset", dF[:], 0.0, writes=[b_dF])
            P.op("pool", "memset", dB[:], 0.0, writes=[b_dB])

            def two(name, shape, dt):
                ts = [P.sb(es, "%s%d" % (name, i), shape, dt) for i in range(2)]
                return ts, [P.buf(ts[i], "%s%d" % (name, i)) for i in range(2)]
            BT, b_BT = two("sBT", [128, 8, 128], BF16)
            CT, b_CT = two("sCT", [128, 8, 128], BF16)
            Btk, b_Btk = two("sBt", [128, 1024], BF16)
            dtt, b_dtt = two("sdt", [128, 64], F32)
            xsk, b_xsk = two("xsk", [128, 2048], BF16)
            zs, b_zs = two("zsk", [128, 2048], BF16)
            hbc, b_hbc = two("hbc", [128, 2048], BF16)
            sm, b_sm = two("ssm", [128, 320], F32)
            stack, b_stack = two("stack", [128, 128], BF16)
            nstack, b_nstack = two("nstack", [128, 128], BF16)
            cbm, b_cbm = two("cbm", [128, 8, 128], F32)
            dec, b_dec = two("dec", [128, 4, 128], F32)
            M = [P.sb(es, "M%d" % i, [128, 16, 128], BF16) for i in range(2)]
            b_M = [[P.buf(M[i][:, 4 * q:4 * q + 4, :], "M%d_%d" % (i, q)) for q in range(4)] for i in range(2)]
            lo32 = P.sb(es, "lo32", [128, 128], F32)
            b_lo32 = P.buf(lo32, "lo32")
            dring = Ring([0, 1])
            xdt = [P.sb(es, "xdt%d" % i, [128, 2048], BF16) for i in range(3)]
            b_xdt = [P.buf(xdt[i], "xdt%d" % i) for i in range(3)]
            y = P.sb(es, "yy", [128, 2048], F32)
            b_y = [P.buf(y[:, i * 1024:(i + 1) * 1024], "yy%d" % i) for i in range(2)]
            tA = P.sb(es, "tA", [128, 1024], F32)
            tB = P.sb(es, "tB", [128, 1024], F32)
            b_tA, b_tB = P.buf(tA, "tA"), P.buf(tB, "tB")
            yn = P.sb(es, "yn", [128, 2048], BF16)
            b_yn = P.buf(yn, "yn")
            ynT = P.sb(es, "ynT", [128, 16, 128], BF16)
            b_ynT = [P.buf(ynT[:, 8 * i:8 * i + 8, :], "ynT%d" % i) for i in range(2)]
            hf = P.sb(es, "hf", [128, 2048], F32)
            b_hf = [P.buf(hf[:, q * 512:(q + 1) * 512], "hf%d" % q) for q in range(4)]
            hfb = [P.sb(es, "hfb%d" % i, [128, 2048], BF16) for i in range(3)]
            b_hfb = [P.buf(hfb[i], "hfb%d" % i) for i in range(3)]
            ht = [P.sb(es, "hft%d" % i, [128, 512], F32) for i in range(2)]
            b_ht = [P.buf(ht[i], "hft%d" % i) for i in range(2)]
            P.op("pool", "memset", hf[:], 0.0, writes=b_hf)
            P.op("pool", "memset", hfb[0][:], 0.0, writes=[b_hfb[0]])
            v3 = lambda ap: ap.rearrange("p (h e) -> p h e", e=64)

            def chunk(it, d, R, ci, v):
                sl = it % 2
                cur = it % 3
                nxt_h = (it + 1) % 3
                t0 = ci * 128
                s_, b_s = sm[sl], b_sm[sl]
                self.load_tile(c, sl, R, t0, 128)
                P.dma("sp", dtt[sl][:], d["dt"][t0:t0 + 128, :], writes=[b_dtt[sl]])
                P.dma("sp", xsk[sl][:], d["xs"][t0:t0 + 128, :], writes=[b_xsk[sl]])
                P.dma("sp", Btk[sl][:], d["Bt"][t0:t0 + 128, :], writes=[b_Btk[sl]])
                P.dma("sp", BT[sl][:], d["BT"][:, :, t0:t0 + 128], writes=[b_BT[sl]])
                P.dma("sp", CT[sl][:], d["CT"][:, :, t0:t0 + 128], writes=[b_CT[sl]])
                P.dma("sp", hbc[sl][:], d["hb"][ci], writes=[b_hbc[sl]])
                P.dma("sp", zs[sl][:], d["zs"][t0:t0 + 128, :], writes=[b_zs[sl]])
                for (dst_, lo) in ((dF, 0), (dF, 64), (dB, 32), (dB, 96)):
                    src = 0 if dst_ is dF else 32
                    P.op("dve", "tensor_tensor", out=dst_[:, lo:lo + 32], in0=dtt[sl][:, src:src + 32], in1=K["Abc"][:, src:src + 32], op=ALU.mult,
                         reads=[b_dtt[sl], K["b_Abc"]], writes=[b_dF if dst_ is dF else b_dB])
                pb1, pt1 = self.ps.one()
                P.op("pe", "matmul", pt1[:, 0:32], K["Lf"][:], dF[:, 0:32], start=True, stop=True, reads=[K["b_Lf"], b_dF], writes=[pb1])
                P.op("pe", "matmul", pt1[:, 32:64], K["Lb"][:], dB[:, 32:64], start=True, stop=True, reads=[K["b_Lb"], b_dB], writes=[pb1])
                P.op("pe", "matmul", pt1[:, 64:96], K["ones"][:], dF[:, 0:32], start=True, stop=True, reads=[K["b_ones"], b_dF], writes=[pb1])
                pb2, pt2 = self.ps.one()
                P.op("pe", "matmul", pt2[:, 0:128], dF[:], K["Lf"][:], start=True, stop=False, reads=[K["b_Lf"], b_dF], writes=[pb2])
                P.op("pe", "matmul", pt2[:, 0:128], dB[:], K["Lb"][:], start=False, stop=True, reads=[K["b_Lb"], b_dB], writes=[pb2])
                P.op("act", "activation", out=s_[:, 0:96], in_=pt1[:, 0:96], func=AF.Copy, reads=[pb1], writes=[b_s])
                P.op("dve", "tensor_tensor", out=s_[:, 224:256], in0=s_[:, 64:96], in1=s_[:, 0:32], op=ALU.subtract, reads=[b_s], writes=[b_s])
                P.op("act", "activation", out=s_[:, 224:256], in_=s_[:, 224:256], func=AF.Exp, reads=[b_s], writes=[b_s])
                P.op("dve", "tensor_tensor", out=s_[:, 224:256], in0=s_[:, 224:256], in1=dtt[sl][:, 0:32], op=ALU.mult, reads=[b_s, b_dtt[sl]], writes=[b_s])
                P.op("act", "activation", out=s_[:, 256:288], in_=s_[:, 64:96], func=AF.Exp, reads=[b_s], writes=[b_s])
                P.op("act", "activation", out=s_[:, 96:160], in_=s_[:, 0:64], func=AF.Exp, reads=[b_s], writes=[b_s])
                P.op("dve", "tensor_tensor", out=v3(xdt[2][:]), in0=v3(xsk[sl][:]), in1=s_[:, 224:256].unsqueeze(2).to_broadcast([128, 32, 64]),
                     op=ALU.mult, reads=[b_xsk[sl], b_s], writes=[b_xdt[2]])
                P.op("act", "activation", out=stack[sl][:], in_=pt2[:, 0:128], func=AF.Copy, reads=[pb2], writes=[b_stack[sl]])
                P.op("dve", "tensor_tensor", out=lo32[64:128, :], in0=pt2[64:128, 0:128], in1=stack[sl][64:128, :], op=ALU.subtract,
                     reads=[pb2, b_stack[sl]], writes=[b_lo32])
                P.op("dve", "tensor_copy", out=stack[sl][64:128, :], in_=lo32[64:128, :], reads=[b_lo32], writes=[b_stack[sl]])
                P.op("dve", "tensor_scalar", out=nstack[sl][:], in0=stack[sl][:], scalar1=-1.0, scalar2=None, op0=ALU.mult,
                     reads=[b_stack[sl]], writes=[b_nstack[sl]])
                for q in range(4):
                    pb, pt = self.ps.one()
                    for gi in range(2):
                        g = 2 * q + gi
                        P.op("pe", "matmul", pt[:, gi * 256:(gi + 1) * 256], Btk[sl][:, g * 128:(g + 1) * 128], xdt[2][:, g * 256:(g + 1) * 256],
                             start=True, stop=True, reads=[b_Btk[sl], b_xdt[2]], writes=[pb])
                    hv = hf[:, q * 512:(q + 1) * 512]
                    P.op("dve", "tensor_tensor", out=v3(ht[q % 2][:]), in0=v3(hv), in1=s_[:, 256 + 8 * q:256 + 8 * q + 8].unsqueeze(2).to_broadcast([128, 8, 64]),
                         op=ALU.mult, reads=[b_hf[q], b_s], writes=[b_ht[q % 2]])
                    P.op("dve", "tensor_tensor", out=hv, in0=ht[q % 2][:], in1=pt, op=ALU.add, reads=[b_ht[q % 2], pb], writes=[b_hf[q]])
                P.op("act", "activation", out=hfb[nxt_h][:], in_=hf[:], func=AF.Copy, reads=b_hf, writes=[b_hfb[nxt_h]])
                yield
                for b2 in range(2):
                    pb, pt = self.ps.one()
                    for gi in range(4):
                        g = 4 * b2 + gi
                        P.op("pe", "matmul", pt[:, gi * 128:(gi + 1) * 128], BT[sl][:, g, :], CT[sl][:, g, :], start=True, stop=True,
                             reads=[b_BT[sl], b_CT[sl]], writes=[pb])
                    pv = pt.rearrange("p (g i) -> p g i", i=128)
                    P.op("act", "activation", out=cbm[sl][:, 4 * b2:4 * b2 + 4, :], in_=pv, func=AF.Copy, reads=[pb], writes=[b_cbm[sl]])
                for k_ in range(2):
                    P.op("dve" if (k_ == 0 or XV & 1) else "pool", "tensor_tensor", out=v3(xdt[k_][:]), in0=v3(xsk[sl][:]),
                         in1=dtt[sl][:, 32 * k_:32 * k_ + 32].unsqueeze(2).to_broadcast([128, 32, 64]), op=ALU.mult,
                         reads=[b_xsk[sl], b_dtt[sl]], writes=[b_xdt[k_]])
                yield
                for hf_ in range(2):
                    n_m = 0
                    for dr in range(2):
                        for gl in range(4):
                            g = 4 * hf_ + gl
                            pb, pt = self.ps.one()
                            nk = "negF" if dr == 0 else "negB"
                            P.op("pe", "matmul", pt, self.idb[:], K[nk][:].rearrange("p a b -> p (a b)"), start=True, stop=False,
                                 reads=[self.b_idb, K["b_" + nk]], writes=[pb])
                            for k_ in range(4):
                                hh = dr * 32 + g * 4 + k_
                                o = pt[:, k_ * 128:(k_ + 1) * 128]
                                P.op("pe", "matmul", o, sel[:, hh, :], stack[sl][:], start=False, stop=False,
                                     reads=[b_sel, b_stack[sl]], writes=[pb])
                                P.op("pe", "matmul", o, nstack[sl][:], sel[:, hh, :], start=False, stop=(k_ == 3),
                                     reads=[b_sel, b_nstack[sl]], writes=[pb])
                            kd = dring.next()
                            P.op("act", "activation", out=dec[kd][:].rearrange("p a b -> p (a b)"), in_=pt, func=AF.Exp, reads=[pb], writes=[b_dec[kd]])
                            P.op("pool" if (n_m % 4 == 3 and not XV & 2) else "dve", "tensor_tensor", out=M[dr][:, 4 * gl:4 * gl + 4, :], in0=dec[kd][:],
                                 in1=cbm[sl][:, g:g + 1, :].to_broadcast([128, 4, 128]), op=ALU.mult,
                                 reads=[b_dec[kd], b_cbm[sl]], writes=[b_M[dr][gl]])
                            n_m += 1
                    yield
                    pbd, ptd = self.ps.two()
                    for hl in range(16):
                        h = 16 * hf_ + hl
                        o = ptd[:, hl // 8, (hl % 8) * 64:(hl % 8 + 1) * 64]
                        P.op("pe", "matmul", o, M[0][:, hl, :], xdt[0][:, h * 64:(h + 1) * 64], start=True, stop=False,
                             reads=[b_M[0][hl // 4], b_xdt[0]], writes=[pbd[hl // 8]])
                        P.op("pe", "matmul", o, M[1][:, hl, :], xdt[1][:, h * 64:(h + 1) * 64], start=False, stop=False,
                             reads=[b_M[1][hl // 4], b_xdt[1]], writes=[pbd[hl // 8]])
                        P.op("pe", "matmul", o, Did[:, h, :], xsk[sl][:, h * 64:(h + 1) * 64], start=False, stop=True,
                             reads=[b_Did, b_xsk[sl]], writes=[pbd[hl // 8]])
                    pbf, ptf = self.ps.two()
                    pbb, ptb_ = self.ps.two()
                    for gl in range(4):
                        g = 4 * hf_ + gl
                        P.op("pe", "matmul", ptf[:, gl // 2, (gl % 2) * 256:(gl % 2 + 1) * 256], CT[sl][:, g, :], hfb[cur][:, g * 256:(g + 1) * 256],
                             start=True, stop=True, reads=[b_CT[sl], b_hfb[cur]], writes=[pbf[gl // 2]])
                        P.op("pe", "matmul", ptb_[:, gl // 2, (gl % 2) * 256:(gl % 2 + 1) * 256], CT[sl][:, g, :], hbc[sl][:, g * 256:(g + 1) * 256],
                             start=True, stop=True, reads=[b_CT[sl], b_hbc[sl]], writes=[pbb[gl // 2]])
                    Ef = s_[:, 96 + 16 * hf_:96 + 16 * hf_ + 16].unsqueeze(2).to_broadcast([128, 16, 64])
                    Eb = s_[:, 128 + 16 * hf_:128 + 16 * hf_ + 16].unsqueeze(2).to_broadcast([128, 16, 64])
                    P.op("dve", "tensor_tensor", out=v3(tA[:]), in0=v3(ptf.rearrange("p a b -> p (a b)")), in1=Ef, op=ALU.mult,
                         reads=pbf + [b_s], writes=[b_tA])
                    P.op("dve", "tensor_tensor", out=tA[:], in0=tA[:], in1=ptd.rearrange("p a b -> p (a b)"), op=ALU.add,
                         reads=pbd + [b_tA], writes=[b_tA])
                    P.op("dve", "tensor_tensor", out=v3(tB[:]), in0=v3(ptb_.rearrange("p a b -> p (a b)")), in1=Eb, op=ALU.mult,
                         reads=pbb + [b_s], writes=[b_tB])
                    yh = y[:, hf_ * 1024:(hf_ + 1) * 1024]
                    P.op("dve", "tensor_tensor", out=yh, in0=tA[:], in1=tB[:], op=ALU.add, reads=[b_tA, b_tB], writes=[b_y[hf_]])
                    P.op("pool", "tensor_tensor", out=yh, in0=yh, in1=zs[sl][:, hf_ * 1024:(hf_ + 1) * 1024], op=ALU.mult,
                         reads=[b_y[hf_], b_zs[sl]], writes=[b_y[hf_]])
                    for gl in range(4):
                        g = 4 * hf_ + gl
                        P.op("act", "activation", out=c["junk"][:, gl * 256:(gl + 1) * 256], in_=y[:, g * 256:(g + 1) * 256], func=AF.Square,
                             accum_out=s_[:, 288 + g:289 + g], reads=[b_y[hf_]], writes=[c["b_junk"], b_s])
                    yield
                if LNEXP:
                    P.op("act", "activation", out=s_[:, 296:304], in_=s_[:, 288:296], func=AF.Ln, scale=1.0 / 256, bias=EPS, reads=[b_s], writes=[b_s])
                    P.op("act", "activation", out=s_[:, 304:312], in_=s_[:, 296:304], func=AF.Exp, scale=-0.5, reads=[b_s], writes=[b_s])
                else:
                    P.op("act", "activation", out=s_[:, 296:304], in_=s_[:, 288:296], func=AF.Sqrt, scale=1.0 / 256, bias=EPS, reads=[b_s], writes=[b_s])
                    P.op("dve", "reciprocal", out=s_[:, 304:312], in_=s_[:, 296:304], reads=[b_s], writes=[b_s])
                P.op("dve", "tensor_tensor", out=yn[:].rearrange("p (g e) -> p g e", e=256), in0=y[:].rearrange("p (g e) -> p g e", e=256),
                     in1=s_[:, 304:312].unsqueeze(2).to_broadcast([128, 8, 256]), op=ALU.mult, reads=b_y + [b_s], writes=[b_yn])
                for hh2 in range(2):
                    pb, pt = self.ps.one()
                    ptb16 = pt.bitcast(BF16).rearrange("p (c t) -> p c t", t=128)
                    for q in range(8):
                        j = hh2 * 8 + q
                        P.op("pe", "transpose", ptb16[:, q, :], yn[:, j * 128:(j + 1) * 128], self.idb[:], reads=[b_yn, self.b_idb], writes=[pb])
                    if hh2 == 0:
                        P.op("act", "activation", out=ynT[:, hh2 * 8:(hh2 + 1) * 8, :], in_=ptb16, func=AF.Copy, reads=[pb], writes=[b_ynT[hh2]])
                    else:
                        P.op("dve", "tensor_copy", out=ynT[:, hh2 * 8:(hh2 + 1) * 8, :], in_=ptb16, reads=[pb], writes=[b_ynT[hh2]])
                yield
                pbs, pto = self.ps.two()
                for h2 in range(2):
                    for j in range(16):
                        P.op("pe", "matmul", pto[:, h2, :], ynT[:, j, :], wo3[:, j, h2 * 512:(h2 + 1) * 512], start=(j == 0), stop=(j == 15),
                             reads=[b_ynT[j // 8], b_wo3], writes=[pbs[h2]])
                self.epilogue(c, pbs, pto, c["xt"][sl][0], c["b_xt"][sl][0], mod, v)
                self.store_tile(c, sl, R, t0, 128)

            gens = []
            it = 0
            for (nm, R, L, v) in seqs:
                for ci in range(L // 128):
                    gens.append(chunk(it, scr[nm], R, ci, v))
                    it += 1
            run_pipeline(gens, 3 if XV & 4 else 4)
            P.barrier()
            P.release(mk)

    def layer(self, l):
        I = self.I
        last = (l == 1)
        first_src_x = I["x"] if l == 0 else self.R
        first_src_c = I["ctx"] if l == 0 else self.Rc
        st0 = [(first_src_x, self.R, self.S, 0)]
        st1 = [(self.R, self.out if last else self.R, self.S, 0)]
        if not last:
            st0.append((first_src_c, self.Rc, self.CL, 1))
            st1.append((self.Rc, self.Rc, self.CL, 1))
        if self.on("ffn%d0" % l):
            self.ffn(l, 0, st0, nxt=(0, 1) if l == 0 else (1, 1))
        elif l == 0:
            P = self.P
            b_rc = P.buf(None, "rcopy")
            for t0 in range(0, self.S, 512):
                P.dma("sp", self.R[t0:t0 + 512, :], I["x"][t0:t0 + 512, :], writes=[b_rc])
            P.dma("sp", self.Rc[:, :], I["ctx"][:, :], writes=[b_rc])
            P.barrier()
        if l == 0 and self.on("ssd"):
            self.cast_ffn_weights(1, 0)
            if BGCAST and "gm" in getattr(self, "bgbuf", {}):
                self.bg_cast("ada1", self.ada_bf[1], I["ada_w"][1], 128)
                self.bg_cast("gm", self.gm_win_bf, I["gm_w_in"], 128)
                self.bg_cast("gm", self.gm_wout_bf, I["gm_w_out"], 256)
            self.ssd(l)
        if l == 1 and self.on("gmlp"):
            self.gmlp(l, self.R, self.S)
        if self.on("ffn%d1" % l):
            self.ffn(l, 1, st1)
        elif last:
            P = self.P
            b_fin = P.buf(None, "fin")
            for t0 in range(0, self.S, 512):
                P.dma("sp", self.out[t0:t0 + 512, :], self.R[t0:t0 + 512, :], writes=[b_fin])
            P.barrier()


def host_inputs(inp, b):
    f = np.float32
    m = {}
    m["x"] = np.ascontiguousarray(inp["x"][b])
    m["ctx"] = np.ascontiguousarray(inp["ctx"][b])
    cc = np.stack([inp["c"][b].reshape(8, 128).T, inp["c_ctx"].reshape(8, 128).T], axis=-1)
    m["c_col"] = np.ascontiguousarray(cc.astype(f))
    m["ada_w"] = inp["ada_w"]
    m["ada_b"] = inp["ada_b"]
    m["ada_b_col"] = np.ascontiguousarray(inp["ada_b"].reshape(2, 72, 128).transpose(0, 2, 1))
    m["norm_g"] = inp["norm_g"]
    m["norm_g_col"] = np.ascontiguousarray(inp["norm_g"].reshape(2, 6, 8, 128).transpose(0, 3, 1, 2))
    wi = inp["ffn_w_in"].reshape(2, 2, 8, 128, 2, NJ, 128)
    m["ffn_wi"] = np.ascontiguousarray(wi.transpose(0, 1, 5, 3, 4, 2, 6)).reshape(2, 2, NJ, 128, 2 * 8 * 128)
    wo = inp["ffn_w_out"].reshape(2, 2, NJ, 128, D)
    m["ffn_wo"] = np.ascontiguousarray(wo.transpose(0, 1, 3, 2, 4)).reshape(2, 2, 128, NJ * D)
    m["ssd_w_in"] = inp["ssd_w_in"][0]
    m["ssd_cw"] = np.ascontiguousarray(inp["ssd_conv_w"][0].reshape(5, 32, 128).transpose(2, 1, 0))
    m["ssd_cb"] = np.ascontiguousarray(inp["ssd_conv_b"][0].reshape(32, 128).T)
    m["ssd_dt_bias"] = np.ascontiguousarray(inp["ssd_dt_bias"][0].reshape(1, 64))
    m["ssd_A_log"] = np.ascontiguousarray(inp["ssd_A_log"][0].reshape(1, 64))
    m["ssd_D"] = inp["ssd_D"]
    m["ssd_ng_col"] = np.ascontiguousarray(inp["ssd_norm_g"][0].reshape(16, 128).T)
    m["ssd_w_out"] = inp["ssd_w_out"][0]
    m["gm_w_in"] = inp["gm_w_in"][0]
    m["gm_w_out"] = inp["gm_w_out"][0]
    m["gm_wsT"] = np.ascontiguousarray(inp["gm_w_s"][0].transpose(2, 0, 1))
    m["gm_bsT"] = np.ascontiguousarray(inp["gm_b_s"][0].T)
    m["gm_v_g"] = inp["gm_v_g"]
    m["gm_v_b"] = inp["gm_v_b"]
    return m


_CACHE = {}


def kernel(**inputs):
    inp = {k: np.asarray(v) for k, v in inputs.items()}
    B, S, _ = inp["x"].shape
    CL = inp["ctx"].shape[1]
    key = (S, CL)
    if key not in _CACHE:
        _CACHE[key] = Builder(S, CL).build()
    nc = _CACHE[key]
    shared = None
    in_maps = []
    for b in range(B):
        m = host_inputs(inp, b)
        if shared is None:
            shared = m
        else:
            for k in m:
                if k not in ("x", "ctx", "c_col"):
                    m[k] = shared[k]
        in_maps.append(m)
    res = run_bass_kernel_spmd(nc, in_maps, core_ids=list(range(B)))
    return np.stack([np.asarray(r["out"]) for r in res.results], axis=0).astype(np.float32)
```

```python
import numpy as np
from contextlib import ExitStack
import concourse.bass as bass
import concourse.mybir as mybir
from concourse.bass_utils import run_bass_kernel_spmd

F32 = mybir.dt.float32
BF16 = mybir.dt.bfloat16
AF = mybir.ActivationFunctionType
ALU = mybir.AluOpType
AX = mybir.AxisListType

D = 1024
FF = 2816
NJ = FF // 128
EPS = 1e-6
NCORES = 8
LNEXP = False
import os
XV = int(os.environ.get("XV", "3"))
GV = int(os.environ.get("GV", "0"))
SCHED_MODE = 9


class Buf:
    __slots__ = ("ap", "name", "sem", "cnt", "kind")

    def __init__(self, ap, name=""):
        self.ap = ap
        self.name = name
        self.sem = None
        self.cnt = 0
        self.kind = None


def _elems(ap):
    n = 1
    for d in ap.shape[1:]:
        n *= int(d)
    return n


class Prog:
    ENG = ("pe", "act", "dve", "pool", "sp")

    def __init__(self, nc, es):
        self.nc = nc
        self.es = es
        self.eng = {"pe": nc.tensor, "act": nc.scalar, "dve": nc.vector, "pool": nc.gpsimd, "sp": nc.sync}
        self.ops = []
        self.esem = {e: es.enter_context(nc.semaphore("s_" + e)) for e in ("pe", "act", "dve", "pool")}
        self.bufs = []
        self.nsem = 0
        self.sempool = {"sw": [], "hw": []}
        self.dirty = set()
        self.nsb = 0

    def buf(self, ap, name=""):
        b = Buf(ap, name)
        self.bufs.append(b)
        return b

    def sb(self, stack, name, shape, dt):
        self.nsb += 1
        return stack.enter_context(self.nc.sbuf_tensor("%s_u%d" % (name, self.nsb), list(shape), dt))

    def _bsem(self, b, q):
        kind = "sw" if q == "pool" else "hw"
        if b.sem is not None:
            assert b.kind == kind, "buffer %s mixes SW and HW DMA queues" % b.name
        if b.sem is None:
            b.kind = kind
            if self.sempool[kind]:
                b.sem, b.cnt = self.sempool[kind].pop()
            else:
                b.sem = self.es.enter_context(self.nc.semaphore("b%d" % self.nsem))
                self.nsem += 1
        return b.sem

    def mark(self):
        return len(self.bufs)

    def release(self, mark):
        for b in self.bufs[mark:]:
            if b.sem is not None:
                self.sempool[b.kind].append((b.sem, b.cnt))
                b.sem = None

    def op(self, eng, meth, *args, reads=(), writes=(), **kw):
        out = kw.get("out", args[0] if args else None)
        n = _elems(out)
        if eng == "pe":
            fp32 = len(args) > 1 and args[1].dtype == F32
            dur = 0.03 + n * (4 if fp32 else 1) / 2400.0
        elif eng == "act":
            dur = 0.25 + n / 1150.0
        elif eng == "dve":
            dur = 0.08 + n / 900.0
        else:
            dur = 0.15 + n / 420.0
        grp = eng == "pe" and kw.get("start", True) is False
        self.ops.append(["c", eng, (meth, args, kw), list(reads), list(writes), dur, grp, None, None])

    def dma(self, q, out, in_, reads=(), writes=(), sig=None):
        sb = sig if sig is not None else (writes[0] if writes else reads[0])
        sem = self._bsem(sb, q)
        sb.cnt += 16
        self.dirty.add(sb)
        nbytes = 1
        for d in out.shape:
            nbytes *= int(d)
        nbytes *= 2 if out.dtype == BF16 else 4
        lat = 2.0 + nbytes / 150e3
        self.ops.append(["d", q, (out, in_), list(reads), list(writes), 1.0 if q == "pool" else 0.12, False, (sem, sb.cnt), lat])

    def barrier(self):
        self.ops.append(["bar", None, [(b.sem, b.cnt) for b in self.dirty]])
        self.dirty = set()

    def _schedule(self, ops):
        import heapq
        n = len(ops)
        unit_of = [0] * n
        units = []
        for i, o in enumerate(ops):
            if o[6] and units and ops[units[-1][-1]][1] == "pe" and units[-1][-1] == self._last_pe:
                units[-1].append(i)
            else:
                units.append([i])
            unit_of[i] = len(units) - 1
            if o[1] == "pe":
                self._last_pe = i
        nu = len(units)
        preds = [dict() for _ in range(nu)]
        sync = [set() for _ in range(n)]
        lw, rd, lastsem = {}, {}, {}
        for i, o in enumerate(ops):
            kind, eng, _, reads, writes = o[0], o[1], o[2], o[3], o[4]
            ui = unit_of[i]

            def edge(j, need_sync):
                uj = unit_of[j]
                if uj != ui:
                    pj = ops[j]
                    lat = (pj[8] if pj[0] == "d" else pj[5]) if True else 0.0
                    preds[ui][uj] = max(preds[ui].get(uj, 0.0), 0.2 if need_sync else 0.0)
                if need_sync and j != i:
                    sync[i].add(j)

            for b in reads:
                j = lw.get(id(b))
                if j is not None:
                    pj = ops[j]
                    edge(j, pj[0] == "d" or kind == "d" or pj[1] != eng or eng != "pe")
            for b in writes:
                j = lw.get(id(b))
                if j is not None:
                    pj = ops[j]
                    edge(j, pj[0] == "d" or kind == "d" or pj[1] != eng or eng != "pe")
                for j in rd.get(id(b), ()):
                    pj = ops[j]
                    edge(j, pj[0] == "d" or kind == "d" or pj[1] != eng or eng != "pe")
            if kind == "d":
                j = lastsem.get(o[7][0])
                if j is not None:
                    edge(j, False)
                lastsem[o[7][0]] = i
            for b in reads:
                rd.setdefault(id(b), []).append(i)
            for b in writes:
                lw[id(b)] = i
                rd[id(b)] = []
        pinned = {2: ("pe",), 3: ("sp",), 4: ("pe", "sp"), 5: ("act", "dve", "pool"), 6: ("pe", "sp", "pool"), 7: ("pool",), 8: ("act",), 9: ("dve",), 10: ("act", "pool"), 11: ("dve", "pool"), 12: ("act", "dve")}.get(SCHED_MODE, ())
        lastu = {}
        for u in range(nu):
            e = ops[units[u][0]][1]
            if e in pinned:
                if e in lastu and lastu[e] not in preds[u]:
                    preds[u][lastu[e]] = 0.0
                lastu[e] = u
        succs = [[] for _ in range(nu)]
        npred = [0] * nu
        for u in range(nu):
            npred[u] = len(preds[u])
            for p in preds[u]:
                succs[p].append(u)
        udur = [sum(ops[i][5] for i in units[u]) for u in range(nu)]
        ueng = [ops[units[u][0]][1] for u in range(nu)]
        ulat = [ops[units[u][-1]][8] if ops[units[u][-1]][0] == "d" else None for u in range(nu)]
        fin = [0.0] * nu
        ready = {e: [] for e in self.ENG}
        avail = {e: [] for e in self.ENG}
        free = {e: 0.0 for e in self.ENG}
        for u in range(nu):
            if npred[u] == 0:
                heapq.heappush(ready[ueng[u]], (0.0, u))
        order = []
        left = nu
        while left:
            best = None
            for e in self.ENG:
                while ready[e] and ready[e][0][0] <= free[e]:
                    heapq.heappush(avail[e], heapq.heappop(ready[e])[1])
                if avail[e]:
                    cand = (free[e], avail[e][0], e, True)
                elif ready[e]:
                    cand = (ready[e][0][0], ready[e][0][1], e, False)
                else:
                    continue
                if best is None or cand[:2] < best[:2]:
                    best = cand
            if SCHED_MODE == 0:
                order = list(range(n))
                break
            st, u, e, from_avail = best
            if from_avail:
                heapq.heappop(avail[e])
            else:
                heapq.heappop(ready[e])
            free[e] = st + udur[u]
            fin[u] = st + (ulat[u] if ulat[u] is not None else udur[u])
            order.extend(units[u])
            left -= 1
            for v in succs[u]:
                npred[v] -= 1
                if npred[v] == 0:
                    rt = max(fin[p] + lat for p, lat in preds[v].items())
                    heapq.heappush(ready[ueng[v]], (rt, v))
        return order, sync

    def emit(self):
        cnt = {e: 0 for e in self.esem}
        waited_c = {e: {} for e in self.eng}
        waited_d = {e: {} for e in self.eng}
        phases, cur = [], []
        for o in self.ops:
            if o[0] == "bar":
                phases.append((cur, o[2]))
                cur = []
            else:
                cur.append(o)
        if cur:
            phases.append((cur, []))
        self._last_pe = -1
        for ops, bar_dma in phases:
            self._last_pe = -1
            order, sync = self._schedule(ops)
            sig = set()
            for i in range(len(ops)):
                for j in sync[i]:
                    if ops[j][0] == "c":
                        sig.add(j)
            last = {}
            for i in order:
                if ops[i][0] == "c":
                    last[ops[i][1]] = i
            sig.update(last.values())
            count_of = {}
            for i in order:
                if ops[i][0] == "c" and i in sig:
                    cnt[ops[i][1]] += 1
                    count_of[i] = cnt[ops[i][1]]
            for i in order:
                o = ops[i]
                e = o[1]
                eng = self.eng[e]
                cd, dd = {}, {}
                for j in sync[i]:
                    pj = ops[j]
                    if pj[0] == "c":
                        cd[pj[1]] = max(cd.get(pj[1], 0), count_of[j])
                    else:
                        sm_, v = pj[7]
                        dd[sm_] = max(dd.get(sm_, 0), v)
                for f, v in cd.items():
                    if waited_c[e].get(f, 0) >= v:
                        continue
                    waited_c[e][f] = v
                    eng.wait_ge(self.esem[f], v)
                for sm_, v in dd.items():
                    if waited_d[e].get(sm_, 0) >= v:
                        continue
                    waited_d[e][sm_] = v
                    eng.wait_ge(sm_, v)
                if o[0] == "c":
                    meth, args, kw = o[2]
                    ins = getattr(eng, meth)(*args, **kw)
                    if i in sig:
                        ins.then_inc(self.esem[e], 1)
                else:
                    out, in_ = o[2]
                    eng.dma_start(out=out, in_=in_).then_inc(o[7][0], 16)
            for e in self.eng:
                eng = self.eng[e]
                for f in self.esem:
                    v = cnt[f]
                    if v > 0 and waited_c[e].get(f, 0) < v:
                        waited_c[e][f] = v
                        eng.wait_ge(self.esem[f], v)
                for sm_, v in bar_dma:
                    if waited_d[e].get(sm_, 0) < v:
                        waited_d[e][sm_] = v
                        eng.wait_ge(sm_, v)


class PsumRing:
    def __init__(self, P, es):
        nc = P.nc
        self.t = es.enter_context(nc.psum_tensor("psum", [128, 8, 512], F32))
        self.banks = [P.buf(self.t[:, i, :], "ps%d" % i) for i in range(8)]
        self.i = 0

    def one(self):
        b = self.banks[self.i]
        k = self.i
        self.i = (self.i + 1) % 8
        return b, self.t[:, k, :]

    def two(self):
        if self.i % 2:
            self.i = (self.i + 1) % 8
        k = self.i
        self.i = (self.i + 2) % 8
        return [self.banks[k], self.banks[k + 1]], self.t[:, k:k + 2, :]


def run_pipeline(gens, interval):
    active = []
    nxt = 0
    r = 0
    while nxt < len(gens) or active:
        if nxt < len(gens) and r % interval == 0:
            active.append(gens[nxt])
            nxt += 1
        for g in list(active):
            try:
                next(g)
            except StopIteration:
                active.remove(g)
        r += 1


class Ring:
    def __init__(self, items):
        self.items = items
        self.i = 0

    def next(self):
        x = self.items[self.i]
        self.i = (self.i + 1) % len(self.items)
        return x


class Builder:
    def __init__(self, S, CL, stages=None, debug=None):
        self.S, self.CL = S, CL
        self.stages = stages
        self.debug = debug or []

    def dram_in(self, name, shape, dt=F32):
        return self.nc.dram_tensor(name, list(shape), dt, kind="ExternalInput").ap()

    def dram_tmp(self, name, shape, dt):
        return self.nc.dram_tensor(name, list(shape), dt, kind="Internal").ap()

    def on(self, name):
        return self.stages is None or name in self.stages

    def build(self):
        nc = bass.Bass("TRN2", target_bir_lowering=False)
        self.nc = nc
        S, CL = self.S, self.CL
        I = {}
        I["x"] = self.dram_in("x", [S, D])
        I["ctx"] = self.dram_in("ctx", [CL, D])
        I["c_col"] = self.dram_in("c_col", [128, 8, 2])
        I["ada_w"] = self.dram_in("ada_w", [2, D, 9 * D])
        I["ada_b"] = self.dram_in("ada_b", [2, 9 * D])
        I["ada_b_col"] = self.dram_in("ada_b_col", [2, 128, 72])
        I["norm_g"] = self.dram_in("norm_g", [2, 6, D])
        I["norm_g_col"] = self.dram_in("norm_g_col", [2, 128, 6, 8])
        I["ffn_wi"] = self.dram_in("ffn_wi", [2, 2, NJ, 128, 2 * 8 * 128])
        I["ffn_wo"] = self.dram_in("ffn_wo", [2, 2, 128, NJ * D])
        I["ssd_w_in"] = self.dram_in("ssd_w_in", [D, 6208])
        I["ssd_cw"] = self.dram_in("ssd_cw", [128, 32, 5])
        I["ssd_cb"] = self.dram_in("ssd_cb", [128, 32])
        I["ssd_dt_bias"] = self.dram_in("ssd_dt_bias", [1, 64])
        I["ssd_A_log"] = self.dram_in("ssd_A_log", [1, 64])
        I["ssd_D"] = self.dram_in("ssd_D", [1, 32])
        I["ssd_ng_col"] = self.dram_in("ssd_ng_col", [128, 16])
        I["ssd_w_out"] = self.dram_in("ssd_w_out", [2048, D])
        I["gm_w_in"] = self.dram_in("gm_w_in", [D, 4096])
        I["gm_w_out"] = self.dram_in("gm_w_out", [2048, D])
        I["gm_wsT"] = self.dram_in("gm_wsT", [128, 8, 128])
        I["gm_bsT"] = self.dram_in("gm_bsT", [128, 8])
        I["gm_v_g"] = self.dram_in("gm_v_g", [1, 2048])
        I["gm_v_b"] = self.dram_in("gm_v_b", [1, 2048])
        self.I = I
        self.out = nc.dram_tensor("out", [S, D], F32, kind="ExternalOutput").ap()
        self.dbg = {n: nc.dram_tensor("dbg_" + n, list(shp), dt, kind="ExternalOutput").ap() for n, shp, dt in self.debug}
        self.R = self.dram_tmp("R", [S, D], F32)
        self.Rc = self.dram_tmp("Rc", [CL, D], F32)
        self.wi_bf = self.dram_tmp("wi_bf", [2, 2, NJ, 128, 2 * 8 * 128], BF16)
        self.wo_bf = self.dram_tmp("wo_bf", [2, 2, 128, NJ * D], BF16)

        with ExitStack() as es:
            P = Prog(nc, es)
            self.P = P
            self.ps = PsumRing(P, es)
            self.consts(es)
            self.prepass()
            P.barrier()
            for l in range(2):
                self.layer(l)
            b_fin = P.buf(None, "fin")
            if "Rc" in self.dbg:
                P.dma("sp", self.dbg["Rc"][:, :], self.Rc[:, :], writes=[b_fin])
            for n in self.dbg:
                if "_" in n:
                    a, b = n.split("_")
                    src = self.scr[b][a]
                    P.dma("sp", self.dbg[n], src, writes=[b_fin])
            P.barrier()
            P.emit()
        return nc

    def consts(self, es):
        P = self.P
        idf = P.sb(es, "idf", [128, 128], F32)
        idb = P.sb(es, "idb", [128, 128], BF16)
        self.b_idf = P.buf(idf[:], "idf")
        self.b_idb = P.buf(idb[:], "idb")
        self.idf, self.idb = idf, idb
        P.op("pool", "memset", idf[:], 1.0, writes=[self.b_idf])
        P.op("pool", "affine_select", out=idf[:], in_=idf[:], pattern=[[1, 128]], compare_op=ALU.is_equal,
             fill=0.0, base=0, channel_multiplier=-1, reads=[self.b_idf], writes=[self.b_idf])
        P.op("dve", "tensor_copy", out=idb[:], in_=idf[:], reads=[self.b_idf], writes=[self.b_idb])

    def cast_ffn_weights(self, l, s):
        P, I = self.P, self.I
        if not self.on("ffn%d%d" % (l, s)):
            return
        for j0 in range(0, NJ, 2):
            P.dma("pool", self.wi_bf[l, s, j0:j0 + 2].rearrange("j p n -> (j p) n"),
                  I["ffn_wi"][l, s, j0:j0 + 2].rearrange("j p n -> (j p) n"), writes=[self.b_wcast])
        for h in range(4):
            n = NJ * D // 4
            P.dma("pool", self.wo_bf[l, s, :, h * n:(h + 1) * n], I["ffn_wo"][l, s, :, h * n:(h + 1) * n],
                  writes=[self.b_wcast])

    def prepass(self):
        P = self.P
        self.b_wcast = P.buf(None, "wcast")
        self.cast_ffn_weights(0, 0)

    def prep_mod(self, es, l, s, want_ctx):
        P, I = self.P, self.I
        gs_col = P.sb(es, "gs_col", [128, 8, 2], F32)
        sh_col = P.sb(es, "sh_col", [128, 8, 2], F32)
        gg = P.sb(es, "gg", [128, 2, 1024], F32)
        with ExitStack() as es2:
            mk = P.mark()
            r = self._prep_mod(es2, l, s, want_ctx, gs_col, sh_col, gg)
            P.barrier()
            P.release(mk)
        return r

    def _prep_mod(self, es, l, s, want_ctx, gs_col, sh_col, gg):
        P, I = self.P, self.I
        cc = P.sb(es, "cc", [128, 8, 2], F32)
        sc = P.sb(es, "sc", [128, 8, 2], BF16)
        screp = P.sb(es, "screp", [128, 8, 2, 128], BF16)
        aw = [P.sb(es, "aw%d" % i, [128, 8, 1024], BF16) for i in range(2)]
        bcol = P.sb(es, "bcol", [128, 72], F32)
        gcol = P.sb(es, "gcol", [128, 6, 8], F32)
        brow = P.sb(es, "brow", [128, 1024], F32)
        grow = P.sb(es, "grow", [128, 1024], F32)
        tmpc = P.sb(es, "tmpc", [128, 8, 2], F32)
        b_cc, b_sc, b_screp = P.buf(cc, "cc"), P.buf(sc, "sc"), P.buf(screp, "screp")
        b_aw = [P.buf(aw[i], "aw%d" % i) for i in range(2)]
        b_bcol, b_gcol, b_brow, b_grow = P.buf(bcol, "bcol"), P.buf(gcol, "gcol"), P.buf(brow, "brow"), P.buf(grow, "grow")
        b_gs, b_sh, b_gg, b_tmpc = P.buf(gs_col, "gs"), P.buf(sh_col, "sh"), P.buf(gg, "gg"), P.buf(tmpc, "tmpc")
        P.dma("sp", cc[:], I["c_col"][:, :, :], writes=[b_cc])
        P.dma("sp", bcol[:], I["ada_b_col"][l], writes=[b_bcol])
        P.dma("sp", gcol[:], I["norm_g_col"][l], writes=[b_gcol])
        P.dma("sp", brow[:], I["ada_b"][l:l + 1, (3 * s + 2) * D:(3 * s + 3) * D].to_broadcast([128, D]), writes=[b_brow])
        P.dma("sp", grow[:], I["norm_g"][l, 2 * s + 1:2 * s + 2, :].to_broadcast([128, D]), writes=[b_grow])
        P.op("act", "activation", out=sc[:], in_=cc[:], func=AF.Silu, reads=[b_cc], writes=[b_sc])
        P.op("dve", "tensor_copy", out=screp[:], in_=sc[:].unsqueeze(3).to_broadcast([128, 8, 2, 128]),
             reads=[b_sc], writes=[b_screp])
        wgt = 0.5 if s != 1 else 1.0
        for r in range(3):
            m = 3 * s + r
            a, b_a = aw[r % 2], b_aw[r % 2]
            P.dma("pool", a[:], I["ada_w"][l, :, m * D:(m + 1) * D].rearrange("(kc p) n -> p kc n", p=128), writes=[b_a])
            if r < 2:
                pb, pt = self.ps.one()
                pv = pt[:, 0:16].rearrange("p (c v) -> p c v", v=2)
                for dc in range(8):
                    for kc in range(8):
                        P.op("pe", "matmul", pv[:, dc, :], a[:, kc, dc * 128:(dc + 1) * 128], sc[:, kc, :],
                             start=(kc == 0), stop=(kc == 7), reads=[b_a, b_sc], writes=[pb])
                bc = bcol[:, m * 8:(m + 1) * 8].unsqueeze(2).to_broadcast([128, 8, 2])
                if r == 0:
                    P.op("dve", "tensor_tensor", out=sh_col[:], in0=pv, in1=bc, op=ALU.add,
                         reads=[pb, b_bcol], writes=[b_sh])
                else:
                    P.op("dve", "tensor_tensor", out=tmpc[:], in0=pv, in1=bc, op=ALU.add,
                         reads=[pb, b_bcol], writes=[b_tmpc])
                    gc = gcol[:, 2 * s, :].unsqueeze(2).to_broadcast([128, 8, 2])
                    P.op("dve", "scalar_tensor_tensor", out=gs_col[:], in0=tmpc[:], scalar=1.0, in1=gc,
                         op0=ALU.add, op1=ALU.mult, reads=[b_tmpc, b_gcol], writes=[b_gs])
            else:
                for v in range(2 if want_ctx else 1):
                    for h in range(2):
                        pb, pt = self.ps.one()
                        for kc in range(8):
                            P.op("pe", "matmul", pt, screp[:, kc, v, :], a[:, kc, h * 512:(h + 1) * 512],
                                 start=(kc == 0), stop=(kc == 7), reads=[b_a, b_screp], writes=[pb])
                        o = gg[:, v, h * 512:(h + 1) * 512]
                        P.op("dve", "tensor_tensor", out=o, in0=pt, in1=brow[:, h * 512:(h + 1) * 512], op=ALU.add,
                             reads=[pb, b_brow], writes=[b_gg])
                        P.op("dve", "scalar_tensor_tensor", out=o, in0=o, scalar=wgt, in1=grow[:, h * 512:(h + 1) * 512],
                             op0=ALU.mult, op1=ALU.mult, reads=[b_gg, b_grow], writes=[b_gg])
        return dict(gs=gs_col, sh=sh_col, gg=gg, b_gs=b_gs, b_sh=b_sh, b_gg=b_gg)

    def alloc_common(self, es, nsub, lite=False):
        P = self.P
        c = {}
        c["xt"] = [[P.sb(es, "xt%d_%d" % (i, s), [128, D], F32) for s in range(nsub)] for i in range(2)]
        c["b_xt"] = [[P.buf(c["xt"][i][s], "xt%d_%d" % (i, s)) for s in range(nsub)] for i in range(2)]
        c["junk"] = P.sb(es, "junk", [128, D], BF16)
        c["b_junk"] = P.buf(c["junk"], "junk")
        c["st"] = P.sb(es, "st", [128, 64], F32)
        c["b_st"] = [P.buf(c["st"][:, 4 * i:4 * i + 4], "st%d" % i) for i in range(16)]
        c["st_ring"] = Ring(list(range(16)))
        ntmp = 1 if lite else 2
        c["tmp"] = [P.sb(es, "tmp%d" % i, [128, D], F32) for i in range(ntmp)]
        c["b_tmp"] = [P.buf(c["tmp"][i], "tmp%d" % i) for i in range(ntmp)]
        c["tmp_ring"] = Ring(list(range(ntmp)))
        if lite:
            return c
        c["xs"] = [P.sb(es, "xs%d" % i, [128, D], BF16) for i in range(2)]
        c["b_xs"] = [P.buf(c["xs"][i], "xs%d" % i) for i in range(2)]
        c["xs_ring"] = Ring([0, 1])
        c["xnT"] = [P.sb(es, "xnT%d" % i, [128, 8, 128 * nsub], BF16) for i in range(2)]
        c["b_xnT"] = [[P.buf(c["xnT"][i][:, :, s * 128:(s + 1) * 128], "xnT%d_%d" % (i, s)) for s in range(nsub)] for i in range(2)]
        return c

    def load_tile(self, c, slot, src, t0, n):
        for s in range(n // 128):
            self.P.dma("sp", c["xt"][slot][s][:], src[t0 + s * 128:t0 + (s + 1) * 128, :], writes=[c["b_xt"][slot][s]])

    def store_tile(self, c, slot, dst, t0, n):
        for s in range(n // 128):
            self.P.dma("sp", dst[t0 + s * 128:t0 + (s + 1) * 128, :], c["xt"][slot][s][:], reads=[c["b_xt"][slot][s]])

    def rstd_of(self, c, src_ap, src_bufs, n_feat):
        P = self.P
        k = c["st_ring"].next()
        st, b_st = c["st"][:, 4 * k:4 * k + 4], c["b_st"][k]
        junk, b_junk = c["junk"], c["b_junk"]
        P.op("act", "activation", out=junk[:, :n_feat], in_=src_ap, func=AF.Square, accum_out=st[:, 0:1],
             reads=list(src_bufs), writes=[b_junk, b_st])
        if c.get("lnexp"):
            P.op("act", "activation", out=st[:, 1:2], in_=st[:, 0:1], func=AF.Ln, scale=1.0 / n_feat, bias=EPS, reads=[b_st], writes=[b_st])
            P.op("act", "activation", out=st[:, 2:3], in_=st[:, 1:2], func=AF.Exp, scale=-0.5, reads=[b_st], writes=[b_st])
            return st[:, 2:3], b_st
        P.op("act", "activation", out=st[:, 1:2], in_=st[:, 0:1], func=AF.Sqrt, scale=1.0 / n_feat, bias=EPS,
             reads=[b_st], writes=[b_st])
        P.op("dve", "reciprocal", out=st[:, 2:3], in_=st[:, 1:2], reads=[b_st], writes=[b_st])
        return st[:, 2:3], b_st

    def prenorm(self, c, slot, n, mod, v):
        for s in range(n // 128):
            self.prenorm_one(c, c["xt"][slot][s], c["b_xt"][slot][s], c["xnT"][slot][:, :, s * 128:(s + 1) * 128], c["b_xnT"][slot][s], mod, v)

    def prenorm_one(self, c, x, b_x, o, b_o, mod, v):
        P = self.P
        if True:
            rstd, b_st = self.rstd_of(c, x[:], [b_x], D)
            k = c["xs_ring"].next()
            xs, b_xs = c["xs"][k], c["b_xs"][k]
            P.op("dve", "tensor_scalar", out=xs[:], in0=x[:], scalar1=rstd, scalar2=None, op0=ALU.mult,
                 reads=[b_x, b_st], writes=[b_xs])
            pb, pt = self.ps.one()
            ptb = pt.bitcast(BF16).rearrange("p (c t) -> p c t", t=128)
            for kc in range(8):
                P.op("pe", "transpose", ptb[:, kc, :], xs[:, kc * 128:(kc + 1) * 128], self.idb[:],
                     reads=[b_xs, self.b_idb], writes=[pb])
            gsb = mod["gs"][:, :, v:v + 1].to_broadcast([128, 8, 128])
            shb = mod["sh"][:, :, v:v + 1].to_broadcast([128, 8, 128])
            k2 = c["tmp_ring"].next()
            tmp, b_tmp = c["tmp"][k2], c["b_tmp"][k2]
            tv = tmp[:].rearrange("p (c t) -> p c t", t=128)
            P.op("dve", "tensor_tensor", out=tv, in0=ptb, in1=gsb, op=ALU.mult, reads=[pb, mod["b_gs"]], writes=[b_tmp])
            P.op("pool", "tensor_tensor", out=o, in0=tv, in1=shb, op=ALU.add, reads=[b_tmp, mod["b_sh"]], writes=[b_o])

    def epilogue(self, c, pbs, pt2, x, b_x, mod, v):
        P = self.P
        yv = pt2.rearrange("p a b -> p (a b)")
        rstd, b_st = self.rstd_of(c, yv, pbs, D)
        k2 = c["tmp_ring"].next()
        tmp, b_tmp = c["tmp"][k2], c["b_tmp"][k2]
        P.op("dve", "tensor_tensor", out=tmp[:], in0=yv, in1=mod["gg"][:, v, :], op=ALU.mult,
             reads=list(pbs) + [mod["b_gg"]], writes=[b_tmp])
        P.op("dve", "scalar_tensor_tensor", out=x[:], in0=tmp[:], scalar=rstd, in1=x[:], op0=ALU.mult, op1=ALU.add,
             reads=[b_tmp, b_st, b_x], writes=[b_x])

    def ffn(self, l, fs, streams, nxt=None):
        P = self.P
        s_idx = 0 if fs == 0 else 2
        with ExitStack() as es:
            mk = P.mark()
            mod = self.prep_mod(es, l, s_idx, any(v == 1 for _, _, _, v in streams))
            if nxt is not None:
                self.cast_ffn_weights(*nxt)
            c = self.alloc_common(es, 4)
            wo = P.sb(es, "wo", [128, NJ, D], BF16)
            b_wo = P.buf(wo, "wo")
            for h in range(2):
                n = NJ // 2
                P.dma("sp", wo[:, h * n:(h + 1) * n, :], self.wo_bf[l, fs, :, h * n * D:(h + 1) * n * D].rearrange("p (j d) -> p j d", d=D),
                      writes=[b_wo])
            wi = [P.sb(es, "wi%d" % i, [128, 2, 8, 128], BF16) for i in range(4)]
            b_wi = [P.buf(wi[i], "wi%d" % i) for i in range(4)]
            wring = Ring([0, 1, 2, 3])
            gT = P.sb(es, "gT", [128, NJ, 512], BF16)
            b_gT = [P.buf(gT[:, j, :], "gT%d" % j) for j in range(NJ)]
            sg = [P.sb(es, "sg%d" % i, [128, 512], F32) for i in range(2)]
            b_sg = [P.buf(sg[i], "sg%d" % i) for i in range(2)]
            sgring = Ring([0, 1])
            tiles = []
            for (R, Rd, n, v) in streams:
                for t0 in range(0, n, 512):
                    tiles.append((R, t0, min(512, n - t0), v, Rd))
            for ti, (R, t0, T, v, Rd) in enumerate(tiles):
                slot = ti % 2
                if ti == 0:
                    self.load_tile(c, slot, R, t0, T)
                    self.prenorm(c, slot, T, mod, v)
                if ti + 1 < len(tiles):
                    R2, t2, T2, v2, _ = tiles[ti + 1]
                    self.load_tile(c, 1 - slot, R2, t2, T2)
                xnT = c["xnT"][slot]
                b_xn = c["b_xnT"][slot][:T // 128]
                for j in range(NJ):
                    k = wring.next()
                    w, b_w = wi[k], b_wi[k]
                    P.dma("sp", w[:].rearrange("p a k f -> p (a k f)"), self.wi_bf[l, fs, j], writes=[b_w])
                    pbg, ptg = self.ps.one()
                    pbu, ptu = self.ps.one()
                    for (pb, pt, gu) in ((pbg, ptg, 0), (pbu, ptu, 1)):
                        for kc in range(8):
                            P.op("pe", "matmul", pt[:, :T], w[:, gu, kc, :], xnT[:, kc, :T], start=(kc == 0), stop=(kc == 7),
                                 reads=[b_w] + b_xn, writes=[pb])
                    k2 = sgring.next()
                    P.op("act", "activation", out=sg[k2][:, :T], in_=ptg[:, :T], func=AF.Silu, reads=[pbg], writes=[b_sg[k2]])
                    P.op("dve", "tensor_tensor", out=gT[:, j, :T], in0=sg[k2][:, :T], in1=ptu[:, :T], op=ALU.mult,
                         reads=[b_sg[k2], pbu], writes=[b_gT[j]])
                    if j == NJ // 2 and ti + 1 < len(tiles):
                        self.prenorm(c, 1 - slot, tiles[ti + 1][2], mod, tiles[ti + 1][3])
                for s in range(T // 128):
                    pbs, pt2 = self.ps.two()
                    for h in range(2):
                        for j in range(NJ):
                            P.op("pe", "matmul", pt2[:, h, :], gT[:, j, s * 128:(s + 1) * 128], wo[:, j, h * 512:(h + 1) * 512],
                                 start=(j == 0), stop=(j == NJ - 1), reads=[b_gT[j], b_wo], writes=[pbs[h]])
                    self.epilogue(c, pbs, pt2, c["xt"][slot][s], c["b_xt"][slot][s], mod, v)
                self.store_tile(c, slot, Rd, t0, T)
            P.barrier()
            P.release(mk)

    def gmlp(self, l, R, n):
        P, I = self.P, self.I
        with ExitStack() as es:
            mk = P.mark()
            mod = self.prep_mod(es, l, 1, False)
            c = self.alloc_common(es, 1)
            xt3 = P.sb(es, "xt3", [128, D], F32)
            xts = [c["xt"][0][0], c["xt"][1][0], xt3]
            b_xts = [c["b_xt"][0][0], c["b_xt"][1][0], P.buf(xt3, "xt3")]
            wg = P.sb(es, "wg", [128, 8, 4096], BF16)
            b_wg = P.buf(wg, "wg")
            for kc in range(8):
                P.dma("pool", wg[:, kc, :], I["gm_w_in"][kc * 128:(kc + 1) * 128, :], writes=[b_wg])
            wo2 = P.sb(es, "wo2", [128, 16, D], BF16)
            b_wo2 = P.buf(wo2, "wo2")
            for q in range(4):
                P.dma("pool", wo2[:, 4 * q:4 * q + 4, :], I["gm_w_out"][q * 512:(q + 1) * 512, :].rearrange("(j p) d -> p j d", p=128),
                      writes=[b_wo2])
            wsT = P.sb(es, "wsT", [128, 8, 128], BF16)
            b_wsT = P.buf(wsT, "wsT")
            P.dma("pool", wsT[:], I["gm_wsT"][:, :, :], writes=[b_wsT])
            bsT = P.sb(es, "bsT", [128, 8], F32)
            b_bsT = P.buf(bsT, "bsT")
            P.dma("sp", bsT[:], I["gm_bsT"][:, :], writes=[b_bsT])
            vg = P.sb(es, "vg", [128, 2048], F32)
            vb = P.sb(es, "vb", [128, 2048], F32)
            b_vg, b_vb = P.buf(vg, "vg"), P.buf(vb, "vb")
            P.dma("sp", vg[:], I["gm_v_g"][0:1, :].to_broadcast([128, 2048]), writes=[b_vg])
            P.dma("sp", vb[:], I["gm_v_b"][0:1, :].to_broadcast([128, 2048]), writes=[b_vb])
            junk2 = P.sb(es, "junk2", [128, 2048], BF16)
            b_junk2 = P.buf(junk2, "junk2")
            guf, gvf, gvn, mm, mT, b_guf, b_gvf, b_gvn, b_mm, b_mT = [], [], [], [], [], [], [], [], [], []
            for i in range(2):
                guf.append(P.sb(es, "guf%d" % i, [128, 2048], BF16))
                gvf.append(P.sb(es, "gvf%d" % i, [128, 2048], F32))
                gvn.append(P.sb(es, "gvn%d" % i, [128, 2048], BF16))
                mm.append(P.sb(es, "mm%d" % i, [128, 2048], BF16))
                mT.append(P.sb(es, "mT%d" % i, [128, 16, 128], BF16))
                b_guf.append([P.buf(guf[i][:, q * 512:(q + 1) * 512], "guf%d_%d" % (i, q)) for q in range(4)])
                b_gvf.append([P.buf(gvf[i][:, q * 512:(q + 1) * 512], "gvf%d_%d" % (i, q)) for q in range(4)])
                b_gvn.append(P.buf(gvn[i], "gvn%d" % i))
                b_mm.append(P.buf(mm[i], "mm%d" % i))
                b_mT.append([P.buf(mT[i][:, 8 * q:8 * q + 8, :], "mT%d_%d" % (i, q)) for q in range(2)])
            lnst = P.sb(es, "lnst", [128, 16], F32)
            b_ln = [P.buf(lnst[:, 8 * i:8 * i + 8], "ln%d" % i) for i in range(2)]
            nt = n // 128

            def chunk(ti):
                slot = ti % 2
                x3 = ti % 3
                t0 = ti * 128
                xt, b_xt = xts[x3], b_xts[x3]
                xnT, b_xn = c["xnT"][slot], c["b_xnT"][slot][0]
                ln, b_l = lnst[:, 8 * slot:8 * slot + 8], b_ln[slot]
                G, V, N_, Mm, MT = guf[slot], gvf[slot], gvn[slot], mm[slot], mT[slot]
                bG, bV, bN, bM, bMT = b_guf[slot], b_gvf[slot], b_gvn[slot], b_mm[slot], b_mT[slot]
                P.dma("sp", xt[:], R[t0:t0 + 128, :], writes=[b_xt])
                self.prenorm_one(c, xt, b_xt, c["xnT"][slot][:, :, 0:128], b_xn, mod, 0)
                yield
                for cb in range(8):
                    pb, pt = self.ps.one()
                    for kc in range(8):
                        P.op("pe", "matmul", pt, xnT[:, kc, :], wg[:, kc, cb * 512:(cb + 1) * 512], start=(kc == 0), stop=(kc == 7),
                             reads=[b_xn, b_wg], writes=[pb])
                    if cb < 4:
                        P.op("act", "activation", out=G[:, cb * 512:(cb + 1) * 512], in_=pt, func=AF.Gelu_apprx_tanh,
                             reads=[pb], writes=[bG[cb]])
                    else:
                        k = cb - 4
                        P.op("act", "activation", out=V[:, k * 512:(k + 1) * 512], in_=pt, func=AF.Gelu_apprx_tanh,
                             accum_out=ln[:, k:k + 1], reads=[pb], writes=[bV[k], b_l])
                yield
                P.op("act", "activation", out=junk2[:], in_=V[:], func=AF.Square, accum_out=ln[:, 4:5],
                     reads=bV, writes=[b_junk2, b_l])
                P.op("dve", "tensor_reduce", out=ln[:, 5:6], in_=ln[:, 0:4], axis=AX.X, op=ALU.add, reads=[b_l], writes=[b_l])
                P.op("dve", "tensor_scalar", out=ln[:, 5:6], in0=ln[:, 5:6], scalar1=1.0 / 2048, scalar2=None, op0=ALU.mult,
                     reads=[b_l], writes=[b_l])
                P.op("dve", "tensor_tensor", out=ln[:, 6:7], in0=ln[:, 5:6], in1=ln[:, 5:6], op=ALU.mult, reads=[b_l], writes=[b_l])
                P.op("dve", "scalar_tensor_tensor", out=ln[:, 6:7], in0=ln[:, 4:5], scalar=1.0 / 2048, in1=ln[:, 6:7],
                     op0=ALU.mult, op1=ALU.subtract, reads=[b_l], writes=[b_l])
                P.op("act", "activation", out=ln[:, 7:8], in_=ln[:, 6:7], func=AF.Sqrt, bias=EPS, reads=[b_l], writes=[b_l])
                P.op("dve", "reciprocal", out=ln[:, 7:8], in_=ln[:, 7:8], reads=[b_l], writes=[b_l])
                for hh in range(2):
                    vs = slice(hh * 1024, (hh + 1) * 1024)
                    bv2 = bV[2 * hh:2 * hh + 2]
                    P.op("dve", "tensor_scalar", out=V[:, vs], in0=V[:, vs], scalar1=ln[:, 5:6], scalar2=ln[:, 7:8], op0=ALU.subtract, op1=ALU.mult,
                         reads=bv2 + [b_l], writes=bv2)
                    P.op("pool" if (hh == 0 and not GV & 2) else "dve", "tensor_tensor", out=V[:, vs], in0=V[:, vs], in1=vg[:, vs], op=ALU.mult, reads=bv2 + [b_vg], writes=bv2)
                    P.op("dve" if (GV & 4 and hh == 1) else "pool", "tensor_tensor", out=N_[:, vs], in0=V[:, vs], in1=vb[:, vs], op=ALU.add, reads=bv2 + [b_vb], writes=[bN])
                yield
                for g2 in range(4):
                    pb, pt = self.ps.one()
                    for gi in range(2):
                        g = 2 * g2 + gi
                        P.op("pe", "matmul", pt[:, gi * 256:(gi + 1) * 256], wsT[:, g, :], N_[:, g * 256:(g + 1) * 256], start=True, stop=True,
                             reads=[b_wsT, bN], writes=[pb])
                    for gi in range(2):
                        g = 2 * g2 + gi
                        P.op("dve", "scalar_tensor_tensor", out=Mm[:, g * 256:(g + 1) * 256], in0=pt[:, gi * 256:(gi + 1) * 256],
                             scalar=bsT[:, g:g + 1], in1=G[:, g * 256:(g + 1) * 256], op0=ALU.add, op1=ALU.mult,
                             reads=[pb, b_bsT, bG[g // 2]], writes=[bM])
                for hh in range(2):
                    pb, pt = self.ps.one()
                    ptb = pt.bitcast(BF16).rearrange("p (c t) -> p c t", t=128)
                    for q in range(8):
                        j = hh * 8 + q
                        P.op("pe", "transpose", ptb[:, q, :], Mm[:, j * 128:(j + 1) * 128], self.idb[:], reads=[bM, self.b_idb], writes=[pb])
                    P.op("act", "activation", out=MT[:, hh * 8:(hh + 1) * 8, :], in_=ptb, func=AF.Copy, reads=[pb], writes=[bMT[hh]])
                yield
                pbs, pt2 = self.ps.two()
                for h in range(2):
                    for j in range(16):
                        P.op("pe", "matmul", pt2[:, h, :], MT[:, j, :], wo2[:, j, h * 512:(h + 1) * 512], start=(j == 0), stop=(j == 15),
                             reads=[bMT[j // 8], b_wo2], writes=[pbs[h]])
                self.epilogue(c, pbs, pt2, xt, b_xt, mod, 0)
                P.dma("sp", R[t0:t0 + 128, :], xt[:], reads=[b_xt])

            run_pipeline([chunk(ti) for ti in range(nt)], 3 if GV & 1 else 2)
            P.barrier()
            P.release(mk)

    def ssd_consts(self, es):
        P, I = self.P, self.I
        k = {}
        for nm in ("Lf", "Lb", "ones"):
            k[nm] = P.sb(es, nm, [128, 128], F32)
            k["b_" + nm] = P.buf(k[nm], nm)
            P.op("pool", "memset", k[nm][:], 1.0, writes=[k["b_" + nm]])
        P.op("pool", "affine_select", out=k["Lf"][:], in_=k["Lf"][:], pattern=[[1, 128]], compare_op=ALU.is_ge, fill=0.0, base=0,
             channel_multiplier=-1, reads=[k["b_Lf"]], writes=[k["b_Lf"]])
        P.op("pool", "affine_select", out=k["Lb"][:], in_=k["Lb"][:], pattern=[[-1, 128]], compare_op=ALU.is_ge, fill=0.0, base=0,
             channel_multiplier=1, reads=[k["b_Lb"]], writes=[k["b_Lb"]])
        ntmp = P.sb(es, "ntmp", [128, 4, 128], F32)
        b_ntmp = P.buf(ntmp, "ntmp")
        for nm, pat, cm in (("negF", [[0, 4], [-1, 128]], 1), ("negB", [[0, 4], [1, 128]], -1)):
            k[nm] = P.sb(es, nm, [128, 4, 128], BF16)
            k["b_" + nm] = P.buf(k[nm], nm)
            P.op("pool", "memset", ntmp[:], -30000.0, writes=[b_ntmp])
            P.op("pool", "affine_select", out=ntmp[:], in_=ntmp[:], pattern=pat, compare_op=ALU.is_gt, fill=0.0, base=0,
                 channel_multiplier=cm, reads=[b_ntmp], writes=[b_ntmp])
            P.op("dve", "tensor_copy", out=k[nm][:], in_=ntmp[:], reads=[b_ntmp], writes=[k["b_" + nm]])
        k["Abc"] = P.sb(es, "Abc", [128, 64], F32)
        k["dtb"] = P.sb(es, "dtb", [128, 64], F32)
        k["Dbc"] = P.sb(es, "Dbc", [128, 32], F32)
        for nm in ("Abc", "dtb", "Dbc"):
            k["b_" + nm] = P.buf(k[nm], nm)
        P.dma("sp", k["Abc"][:], I["ssd_A_log"][0:1, :].to_broadcast([128, 64]), writes=[k["b_Abc"]])
        P.dma("sp", k["dtb"][:], I["ssd_dt_bias"][0:1, :].to_broadcast([128, 64]), writes=[k["b_dtb"]])
        P.dma("sp", k["Dbc"][:], I["ssd_D"][0:1, :].to_broadcast([128, 32]), writes=[k["b_Dbc"]])
        P.op("act", "activation", out=k["Abc"][:], in_=k["Abc"][:], func=AF.Exp, reads=[k["b_Abc"]], writes=[k["b_Abc"]])
        P.op("dve", "tensor_scalar", out=k["Abc"][:], in0=k["Abc"][:], scalar1=-1.0, scalar2=None, op0=ALU.mult,
             reads=[k["b_Abc"]], writes=[k["b_Abc"]])
        return k

    def ssd(self, l):
        P, I = self.P, self.I
        S, CL = self.S, self.CL
        seqs = [("c", self.Rc, CL, 1), ("x", self.R, S, 0)]
        scr = {}
        for nm, _, L, _ in seqs:
            d = {}
            d["raw"] = self.dram_tmp("raw_" + nm, [128, 32, L + 4], BF16)
            d["zs"] = self.dram_tmp("zs_" + nm, [L, 2048], BF16)
            d["dt"] = self.dram_tmp("dt_" + nm, [L, 64], F32)
            d["xs"] = self.dram_tmp("xs_" + nm, [L, 2048], BF16)
            d["Bt"] = self.dram_tmp("Bt_" + nm, [L, 1024], BF16)
            d["BT"] = self.dram_tmp("BT_" + nm, [128, 8, L], BF16)
            d["CT"] = self.dram_tmp("CT_" + nm, [128, 8, L], BF16)
            d["hb"] = self.dram_tmp("hb_" + nm, [L // 128, 128, 2048], BF16)
            scr[nm] = d
        self.scr = scr
        with ExitStack() as es:
            mk0 = P.mark()
            mod = self.prep_mod(es, l, 1, True)
            self.cast_ffn_weights(1, 0)
            K = self.ssd_consts(es)
            if self.on("ssd_s0"):
                self.ssd_s0(mod, K, seqs, scr)
            if self.on("ssd_s1"):
                self.ssd_s1(mod, K, seqs, scr)
            if self.on("ssd_s2"):
                self.ssd_s2(mod, K, seqs, scr)
            P.barrier()
            P.release(mk0)

    def ssd_s0(self, mod, K, seqs, scr):
        P, I = self.P, self.I
        with ExitStack() as es:
            mk = P.mark()
            c = self.alloc_common(es, 4)
            wx = P.sb(es, "wx", [128, 8, 4096], BF16)
            wz = P.sb(es, "wz", [128, 8, 2048], BF16)
            wd = P.sb(es, "wd", [128, 8, 64], BF16)
            b_wx, b_wz, b_wd = P.buf(wx, "wx"), P.buf(wz, "wz"), P.buf(wd, "wd")
            for kc in range(8):
                rows = I["ssd_w_in"][kc * 128:(kc + 1) * 128, :]
                P.dma("pool", wx[:, kc, :], rows[:, 2048:6144], writes=[b_wx])
                P.dma("pool", wz[:, kc, :], rows[:, 0:2048], writes=[b_wz])
                P.dma("pool", wd[:, kc, :], rows[:, 6144:6208], writes=[b_wd])
            zero = P.sb(es, "zero", [128, 32, 2], BF16)
            b_zero = P.buf(zero, "zero")
            P.op("pool", "memset", zero[:], 0.0, writes=[b_zero])
            stg = [P.sb(es, "stg%d" % i, [128, 4, 512], BF16) for i in range(2)]
            b_stg = [P.buf(stg[i], "stg%d" % i) for i in range(2)]
            stg_ring = Ring([0, 1])
            zst = [P.sb(es, "zst%d" % i, [128, 2048], BF16) for i in range(2)]
            b_zst = [P.buf(zst[i], "zst%d" % i) for i in range(2)]
            zst_ring = Ring([0, 1])
            dst = [P.sb(es, "dst%d" % i, [128, 64], F32) for i in range(2)]
            b_dst = [P.buf(dst[i], "dst%d" % i) for i in range(2)]
            dst_ring = Ring([0, 1])
            tiles = []
            for (nm, R, L, v) in seqs:
                d = scr[nm]
                P.dma("sp", d["raw"][:, :, 0:2], zero[:], reads=[b_zero])
                P.dma("sp", d["raw"][:, :, L + 2:L + 4], zero[:], reads=[b_zero])
                for t0 in range(0, L, 512):
                    tiles.append((nm, R, t0, min(512, L - t0), v))
            for ti, (nm, R, t0, T, v) in enumerate(tiles):
                d = scr[nm]
                slot = ti % 2
                if ti == 0:
                    self.load_tile(c, slot, R, t0, T)
                    self.prenorm(c, slot, T, mod, v)
                if ti + 1 < len(tiles):
                    nm2, R2, t2, T2, v2 = tiles[ti + 1]
                    self.load_tile(c, 1 - slot, R2, t2, T2)
                xnT = c["xnT"][slot]
                b_xn = c["b_xnT"][slot][:T // 128]
                for q in range(8):
                    k = stg_ring.next()
                    for u in range(4):
                        cc = 4 * q + u
                        pb, pt = self.ps.one()
                        for kc in range(8):
                            P.op("pe", "matmul", pt[:, :T], wx[:, kc, cc * 128:(cc + 1) * 128], xnT[:, kc, :T], start=(kc == 0), stop=(kc == 7),
                                 reads=[b_wx] + b_xn, writes=[pb])
                        if u % 2 == 0:
                            P.op("act", "activation", out=stg[k][:, u, :T], in_=pt[:, :T], func=AF.Copy, reads=[pb], writes=[b_stg[k]])
                        else:
                            P.op("dve", "tensor_copy", out=stg[k][:, u, :T], in_=pt[:, :T], reads=[pb], writes=[b_stg[k]])
                    P.dma("sp", d["raw"][:, 4 * q:4 * q + 4, 2 + t0:2 + t0 + T], stg[k][:, :, :T], reads=[b_stg[k]])
                    if q == 3 and ti + 1 < len(tiles):
                        self.prenorm(c, 1 - slot, tiles[ti + 1][3], mod, tiles[ti + 1][4])
                for s in range(T // 128):
                    k = zst_ring.next()
                    for cb in range(4):
                        pb, pt = self.ps.one()
                        for kc in range(8):
                            P.op("pe", "matmul", pt, xnT[:, kc, s * 128:(s + 1) * 128], wz[:, kc, cb * 512:(cb + 1) * 512], start=(kc == 0), stop=(kc == 7),
                                 reads=[b_wz, b_xn[s]], writes=[pb])
                        P.op("act", "activation", out=zst[k][:, cb * 512:(cb + 1) * 512], in_=pt, func=AF.Silu, reads=[pb], writes=[b_zst[k]])
                    P.dma("sp", d["zs"][t0 + s * 128:t0 + (s + 1) * 128, :], zst[k][:], reads=[b_zst[k]])
                    k = dst_ring.next()
                    pb, pt = self.ps.one()
                    for kc in range(8):
                        P.op("pe", "matmul", pt[:, 0:64], xnT[:, kc, s * 128:(s + 1) * 128], wd[:, kc, :], start=(kc == 0), stop=(kc == 7),
                             reads=[b_wd, b_xn[s]], writes=[pb])
                    P.op("dve", "tensor_tensor", out=dst[k][:], in0=pt[:, 0:64], in1=K["dtb"][:], op=ALU.add, reads=[pb, K["b_dtb"]], writes=[b_dst[k]])
                    P.op("act", "activation", out=dst[k][:], in_=dst[k][:], func=AF.Exp, reads=[b_dst[k]], writes=[b_dst[k]])
                    P.op("act", "activation", out=dst[k][:], in_=dst[k][:], func=AF.Ln, bias=1.0, reads=[b_dst[k]], writes=[b_dst[k]])
                    P.dma("sp", d["dt"][t0 + s * 128:t0 + (s + 1) * 128, :], dst[k][:], reads=[b_dst[k]])
            P.barrier()
            P.release(mk)

    def ssd_s1(self, mod, K, seqs, scr):
        P, I = self.P, self.I
        with ExitStack() as es:
            mk = P.mark()
            cw = P.sb(es, "cw", [128, 32, 5], F32)
            cb = P.sb(es, "cb", [128, 32], F32)
            b_cw, b_cb = P.buf(cw, "cw"), P.buf(cb, "cb")
            P.dma("sp", cw[:], I["ssd_cw"][:, :, :], writes=[b_cw])
            P.dma("sp", cb[:], I["ssd_cb"][:, :], writes=[b_cb])
            dg = P.sb(es, "dg", [128, 32, 5, 128], BF16)
            b_dg = [P.buf(dg[:, cc], "dg%d" % cc) for cc in range(32)]
            n_ = 0
            for cc in range(32):
                for tap in range(5):
                    P.op("dve" if n_ % 2 == 0 else "pool", "tensor_scalar", out=dg[:, cc, tap, :], in0=self.idf[:], scalar1=cw[:, cc, tap:tap + 1],
                         scalar2=None, op0=ALU.mult, reads=[self.b_idf, b_cw], writes=[b_dg[cc]])
                    n_ += 1
            W = P.sb(es, "W", [128, 32, 516], BF16)
            b_W = [P.buf(W[:, 8 * q:8 * q + 8, :], "W%d" % q) for q in range(4)]
            xcT2 = [P.sb(es, "xcT%d" % i, [128, 32, 512], BF16) for i in range(2)]
            b_xcT2 = [[P.buf(xcT2[i][:, 8 * q:8 * q + 8, :], "xcT%d_%d" % (i, q)) for q in range(4)] for i in range(2)]
            xst = [P.sb(es, "xst%d" % i, [128, 3072], BF16) for i in range(2)]
            b_xst = [[P.buf(xst[i][:, q * 1024:(q + 1) * 1024], "xst%d_%d" % (i, q)) for q in range(3)] for i in range(2)]
            dtt = [P.sb(es, "dtt%d" % i, [128, 64], F32) for i in range(2)]
            b_dtt = [P.buf(dtt[i], "dtt%d" % i) for i in range(2)]
            sm = [P.sb(es, "sm%d" % i, [128, 256], F32) for i in range(2)]
            b_sm = [P.buf(sm[i], "sm%d" % i) for i in range(2)]
            xdd = [P.sb(es, "xdd%d" % i, [128, 2048], BF16) for i in range(2)]
            b_xdd = [P.buf(xdd[i], "xdd%d" % i) for i in range(2)]
            hb = P.sb(es, "hb", [128, 2048], F32)
            b_hb = [P.buf(hb[:, q * 512:(q + 1) * 512], "hb%d" % q) for q in range(4)]
            hbb = [P.sb(es, "hbb%d" % i, [128, 2048], BF16) for i in range(2)]
            b_hbb = [P.buf(hbb[i], "hbb%d" % i) for i in range(2)]
            ht = [P.sb(es, "ht%d" % i, [128, 512], F32) for i in range(2)]
            b_ht = [P.buf(ht[i], "ht%d" % i) for i in range(2)]
            P.op("pool", "memset", hb[:], 0.0, writes=b_hb)
            P.op("pool", "memset", hbb[0][:], 0.0, writes=[b_hbb[0]])
            tiles = []
            for (nm, R, L, v) in seqs:
                T = min(512, L)
                for ti in range(L // T - 1, -1, -1):
                    tiles.append((scr[nm], ti * T, T))

            def conv_load(k):
                d, tt0, T = tiles[k]
                for q in range(4):
                    P.dma("sp", W[:, 8 * q:8 * q + 8, :T + 4], d["raw"][:, 8 * q:8 * q + 8, tt0:tt0 + T + 4], writes=[b_W[q]])

            def conv_piece(k, q):
                d, tt0, T = tiles[k]
                xcT, b_xcT = xcT2[k % 2], b_xcT2[k % 2]
                for cc in range(8 * q, 8 * q + 8):
                    pb, pt = self.ps.one()
                    for tap in range(5):
                        P.op("pe", "matmul", pt[:, :T], dg[:, cc, tap, :], W[:, cc, tap:tap + T], start=(tap == 0), stop=(tap == 4),
                             reads=[b_dg[cc], b_W[cc // 8]], writes=[pb])
                    P.op("act", "activation", out=xcT[:, cc, :T], in_=pt[:, :T], func=AF.Silu, bias=cb[:, cc:cc + 1],
                         reads=[pb, b_cb], writes=[b_xcT[cc // 8]])
                if q == 2:
                    P.dma("sp", d["BT"][:, :, tt0:tt0 + T], xcT[:, 16:24, :T], reads=[b_xcT[2]])
                if q == 3:
                    P.dma("sp", d["CT"][:, :, tt0:tt0 + T], xcT[:, 24:32, :T], reads=[b_xcT[3]])

            conv_load(0)
            for q in range(4):
                conv_piece(0, q)
            chunks = []
            for k, (d, tt0, T) in enumerate(tiles):
                for sc_ in range(T // 128 - 1, -1, -1):
                    chunks.append((k, d, tt0, T, sc_))

            def segA(n_):
                k, d, tt0, T, sc_ = chunks[n_]
                xcT, b_xcT = xcT2[k % 2], b_xcT2[k % 2]
                sl = n_ % 2
                t0 = tt0 + sc_ * 128
                P.dma("sp", dtt[sl][:], d["dt"][t0:t0 + 128, :], writes=[b_dtt[sl]])
                for q in range(3):
                    pb, pt = self.ps.one()
                    ptb = pt.bitcast(BF16).rearrange("p (c t) -> p c t", t=128)
                    for u in range(8):
                        P.op("pe", "transpose", ptb[:, u, :], xcT[:, q * 8 + u, sc_ * 128:(sc_ + 1) * 128], self.idb[:],
                             reads=[b_xcT[q], self.b_idb], writes=[pb])
                    if q == 1:
                        P.op("dve", "tensor_copy", out=xst[sl][:, q * 1024:(q + 1) * 1024], in_=pt.bitcast(BF16), reads=[pb], writes=[b_xst[sl][q]])
                    else:
                        P.op("act", "activation", out=xst[sl][:, q * 1024:(q + 1) * 1024], in_=pt.bitcast(BF16), func=AF.Copy,
                             reads=[pb], writes=[b_xst[sl][q]])
                P.dma("sp", d["xs"][t0:t0 + 128, :], xst[sl][:, 0:2048], reads=b_xst[sl][0:2])
                P.dma("sp", d["Bt"][t0:t0 + 128, :], xst[sl][:, 2048:3072], reads=[b_xst[sl][2]])
                s_ = sm[sl]
                b_s = b_sm[sl]
                P.op("dve", "tensor_tensor", out=s_[:, 0:32], in0=dtt[sl][:, 32:64], in1=K["Abc"][:, 32:64], op=ALU.mult,
                     reads=[b_dtt[sl], K["b_Abc"]], writes=[b_s])
                pb, pt = self.ps.one()
                P.op("pe", "matmul", pt[:, 0:32], K["Lb"][:], s_[:, 0:32], start=True, stop=True, reads=[K["b_Lb"], b_s], writes=[pb])
                P.op("pe", "matmul", pt[:, 32:64], K["ones"][:], s_[:, 0:32], start=True, stop=True, reads=[K["b_ones"], b_s], writes=[pb])
                P.op("act", "activation", out=s_[:, 32:96], in_=pt[:, 0:64], func=AF.Copy, reads=[pb], writes=[b_s])
                P.op("dve", "tensor_tensor", out=s_[:, 96:128], in0=s_[:, 64:96], in1=s_[:, 32:64], op=ALU.subtract, reads=[b_s], writes=[b_s])
                P.op("act", "activation", out=s_[:, 96:128], in_=s_[:, 96:128], func=AF.Exp, reads=[b_s], writes=[b_s])
                P.op("dve", "tensor_tensor", out=s_[:, 96:128], in0=s_[:, 96:128], in1=dtt[sl][:, 32:64], op=ALU.mult, reads=[b_s, b_dtt[sl]], writes=[b_s])
                P.op("act", "activation", out=s_[:, 128:160], in_=s_[:, 64:96], func=AF.Exp, reads=[b_s], writes=[b_s])
                P.op("dve", "tensor_tensor", out=xdd[sl][:].rearrange("p (h e) -> p h e", e=64), in0=xst[sl][:, 0:2048].rearrange("p (h e) -> p h e", e=64),
                     in1=s_[:, 96:128].unsqueeze(2).to_broadcast([128, 32, 64]), op=ALU.mult, reads=b_xst[sl][0:2] + [b_s], writes=[b_xdd[sl]])

            def segB(n_):
                k, d, tt0, T, sc_ = chunks[n_]
                sl = n_ % 2
                ci = tt0 // 128 + sc_
                s_, b_s = sm[sl], b_sm[sl]
                cur, hsl = n_ % 2, (n_ + 1) % 2
                P.dma("sp", d["hb"][ci], hbb[cur][:], reads=[b_hbb[cur]])
                for q in range(4):
                    pb, pt = self.ps.one()
                    for gi in range(2):
                        g = 2 * q + gi
                        P.op("pe", "matmul", pt[:, gi * 256:(gi + 1) * 256], xst[sl][:, 2048 + g * 128:2048 + (g + 1) * 128], xdd[sl][:, g * 256:(g + 1) * 256],
                             start=True, stop=True, reads=[b_xst[sl][2], b_xdd[sl]], writes=[pb])
                    hv = hb[:, q * 512:(q + 1) * 512]
                    P.op("dve", "tensor_tensor", out=ht[q % 2][:].rearrange("p (h e) -> p h e", e=64), in0=hv.rearrange("p (h e) -> p h e", e=64),
                         in1=s_[:, 128 + 8 * q:128 + 8 * q + 8].unsqueeze(2).to_broadcast([128, 8, 64]), op=ALU.mult,
                         reads=[b_hb[q], b_s], writes=[b_ht[q % 2]])
                    P.op("dve", "tensor_tensor", out=hv, in0=ht[q % 2][:], in1=pt, op=ALU.add, reads=[b_ht[q % 2], pb], writes=[b_hb[q]])
                P.op("act", "activation", out=hbb[hsl][:], in_=hb[:], func=AF.Copy, reads=b_hb, writes=[b_hbb[hsl]])

            pend = {}
            segA(0)
            for n_, (k, d, tt0, T, sc_) in enumerate(chunks):
                first_of_tile = (n_ == 0) or (chunks[n_ - 1][0] != k)
                if first_of_tile and k + 1 < len(tiles):
                    conv_load(k + 1)
                    pend[k + 1] = [0, 1, 2, 3]
                left_in_tile = sc_ + 1
                if pend.get(k + 1):
                    npc = (len(pend[k + 1]) + left_in_tile - 1) // left_in_tile
                    for _ in range(npc):
                        conv_piece(k + 1, pend[k + 1].pop(0))
                if n_ + 1 < len(chunks):
                    segA(n_ + 1)
                segB(n_)
            P.barrier()
            P.release(mk)

    def ssd_s2(self, mod, K, seqs, scr):
        P, I = self.P, self.I
        with ExitStack() as es:
            mk = P.mark()
            c = self.alloc_common(es, 1, lite=True)
            c["lnexp"] = LNEXP
            wo3 = P.sb(es, "wo3", [128, 16, D], BF16)
            b_wo3 = P.buf(wo3, "wo3")
            for q in range(4):
                P.dma("pool", wo3[:, 4 * q:4 * q + 4, :], I["ssd_w_out"][q * 512:(q + 1) * 512, :].rearrange("(j p) d -> p j d", p=128), writes=[b_wo3])
            ngc = P.sb(es, "ngc", [128, 16], F32)
            b_ngc = P.buf(ngc, "ngc")
            P.dma("sp", ngc[:], I["ssd_ng_col"][:, :], writes=[b_ngc])
            for j in range(16):
                P.op("dve" if j % 2 else "pool", "tensor_scalar", out=wo3[:, j, :], in0=wo3[:, j, :], scalar1=ngc[:, j:j + 1], scalar2=None, op0=ALU.mult,
                     reads=[b_wo3, b_ngc], writes=[b_wo3])
            sel = P.sb(es, "sel", [128, 64, 128], BF16)
            b_sel = P.buf(sel, "sel")
            Did = P.sb(es, "Did", [128, 32, 128], BF16)
            b_Did = P.buf(Did, "Did")
            with ExitStack() as es2:
                t1 = P.sb(es2, "selt1", [128, 16, 128], F32)
                t2 = P.sb(es2, "selt2", [128, 16, 128], F32)
                b_t1, b_t2 = P.buf(t1, "selt1"), P.buf(t2, "selt2")
                for q in range(4):
                    for (t, b_t, base) in ((t1, b_t1, -16 * q), (t2, b_t2, -16 * q - 64)):
                        P.op("pool", "memset", t[:], 1.0, writes=[b_t])
                        P.op("pool", "affine_select", out=t[:], in_=t[:], pattern=[[-1, 16], [0, 128]], compare_op=ALU.is_equal, fill=0.0,
                             base=base, channel_multiplier=1, reads=[b_t], writes=[b_t])
                    P.op("dve", "tensor_tensor", out=sel[:, 16 * q:16 * q + 16, :], in0=t1[:], in1=t2[:], op=ALU.add, reads=[b_t1, b_t2], writes=[b_sel])
                for h in range(32):
                    P.op("dve", "tensor_scalar", out=Did[:, h, :], in0=self.idf[:], scalar1=K["Dbc"][:, h:h + 1], scalar2=None, op0=ALU.mult,
                         reads=[self.b_idf, K["b_Dbc"]], writes=[b_Did])
                P.barrier()
            dF = P.sb(es, "dF", [128, 128], F32)
            dB = P.sb(es, "dB", [128, 128], F32)
            b_dF, b_dB = P.buf(dF, "dF"), P.buf(dB, "dB")
            P.op("pool", "memset", dF[:], 0.0, writes=[b_dF])
            P.op("pool", "memset", dB[:], 0.0, writes=[b_dB])

            def two(name, shape, dt):
                ts = [P.sb(es, "%s%d" % (name, i), shape, dt) for i in range(2)]
                return ts, [P.buf(ts[i], "%s%d" % (name, i)) for i in range(2)]
            BT, b_BT = two("sBT", [128, 8, 128], BF16)
            CT, b_CT = two("sCT", [128, 8, 128], BF16)
            Btk, b_Btk = two("sBt", [128, 1024], BF16)
            dtt, b_dtt = two("sdt", [128, 64], F32)
            xsk, b_xsk = two("xsk", [128, 2048], BF16)
            zs, b_zs = two("zsk", [128, 2048], BF16)
            hbc, b_hbc = two("hbc", [128, 2048], BF16)
            sm, b_sm = two("ssm", [128, 320], F32)
            stack, b_stack = two("stack", [128, 128], BF16)
            nstack, b_nstack = two("nstack", [128, 128], BF16)
            cbm, b_cbm = two("cbm", [128, 8, 128], F32)
            dec, b_dec = two("dec", [128, 4, 128], F32)
            M = [P.sb(es, "M%d" % i, [128, 16, 128], BF16) for i in range(2)]
            b_M = [[P.buf(M[i][:, 4 * q:4 * q + 4, :], "M%d_%d" % (i, q)) for q in range(4)] for i in range(2)]
            lo32 = P.sb(es, "lo32", [128, 128], F32)
            b_lo32 = P.buf(lo32, "lo32")
            dring = Ring([0, 1])
            xdt = [P.sb(es, "xdt%d" % i, [128, 2048], BF16) for i in range(3)]
            b_xdt = [P.buf(xdt[i], "xdt%d" % i) for i in range(3)]
            y = P.sb(es, "yy", [128, 2048], F32)
            b_y = [P.buf(y[:, i * 1024:(i + 1) * 1024], "yy%d" % i) for i in range(2)]
            tA = P.sb(es, "tA", [128, 1024], F32)
            tB = P.sb(es, "tB", [128, 1024], F32)
            b_tA, b_tB = P.buf(tA, "tA"), P.buf(tB, "tB")
            yn = P.sb(es, "yn", [128, 2048], BF16)
            b_yn = P.buf(yn, "yn")
            ynT = P.sb(es, "ynT", [128, 16, 128], BF16)
            b_ynT = [P.buf(ynT[:, 8 * i:8 * i + 8, :], "ynT%d" % i) for i in range(2)]
            hf = P.sb(es, "hf", [128, 2048], F32)
            b_hf = [P.buf(hf[:, q * 512:(q + 1) * 512], "hf%d" % q) for q in range(4)]
            hfb = [P.sb(es, "hfb%d" % i, [128, 2048], BF16) for i in range(3)]
            b_hfb = [P.buf(hfb[i], "hfb%d" % i) for i in range(3)]
            ht = [P.sb(es, "hft%d" % i, [128, 512], F32) for i in range(2)]
            b_ht = [P.buf(ht[i], "hft%d" % i) for i in range(2)]
            P.op("pool", "memset", hf[:], 0.0, writes=b_hf)
            P.op("pool", "memset", hfb[0][:], 0.0, writes=[b_hfb[0]])
            v3 = lambda ap: ap.rearrange("p (h e) -> p h e", e=64)

            def chunk(it, d, R, ci, v):
                sl = it % 2
                cur = it % 3
                nxt_h = (it + 1) % 3
                t0 = ci * 128
                s_, b_s = sm[sl], b_sm[sl]
                self.load_tile(c, sl, R, t0, 128)
                P.dma("sp", dtt[sl][:], d["dt"][t0:t0 + 128, :], writes=[b_dtt[sl]])
                P.dma("sp", xsk[sl][:], d["xs"][t0:t0 + 128, :], writes=[b_xsk[sl]])
                P.dma("sp", Btk[sl][:], d["Bt"][t0:t0 + 128, :], writes=[b_Btk[sl]])
                P.dma("sp", BT[sl][:], d["BT"][:, :, t0:t0 + 128], writes=[b_BT[sl]])
                P.dma("sp", CT[sl][:], d["CT"][:, :, t0:t0 + 128], writes=[b_CT[sl]])
                P.dma("sp", hbc[sl][:], d["hb"][ci], writes=[b_hbc[sl]])
                P.dma("sp", zs[sl][:], d["zs"][t0:t0 + 128, :], writes=[b_zs[sl]])
                for (dst_, lo) in ((dF, 0), (dF, 64), (dB, 32), (dB, 96)):
                    src = 0 if dst_ is dF else 32
                    P.op("dve", "tensor_tensor", out=dst_[:, lo:lo + 32], in0=dtt[sl][:, src:src + 32], in1=K["Abc"][:, src:src + 32], op=ALU.mult,
                         reads=[b_dtt[sl], K["b_Abc"]], writes=[b_dF if dst_ is dF else b_dB])
                pb1, pt1 = self.ps.one()
                P.op("pe", "matmul", pt1[:, 0:32], K["Lf"][:], dF[:, 0:32], start=True, stop=True, reads=[K["b_Lf"], b_dF], writes=[pb1])
                P.op("pe", "matmul", pt1[:, 32:64], K["Lb"][:], dB[:, 32:64], start=True, stop=True, reads=[K["b_Lb"], b_dB], writes=[pb1])
                P.op("pe", "matmul", pt1[:, 64:96], K["ones"][:], dF[:, 0:32], start=True, stop=True, reads=[K["b_ones"], b_dF], writes=[pb1])
                pb2, pt2 = self.ps.one()
                P.op("pe", "matmul", pt2[:, 0:128], dF[:], K["Lf"][:], start=True, stop=False, reads=[K["b_Lf"], b_dF], writes=[pb2])
                P.op("pe", "matmul", pt2[:, 0:128], dB[:], K["Lb"][:], start=False, stop=True, reads=[K["b_Lb"], b_dB], writes=[pb2])
                P.op("act", "activation", out=s_[:, 0:96], in_=pt1[:, 0:96], func=AF.Copy, reads=[pb1], writes=[b_s])
                P.op("dve", "tensor_tensor", out=s_[:, 224:256], in0=s_[:, 64:96], in1=s_[:, 0:32], op=ALU.subtract, reads=[b_s], writes=[b_s])
                P.op("act", "activation", out=s_[:, 224:256], in_=s_[:, 224:256], func=AF.Exp, reads=[b_s], writes=[b_s])
                P.op("dve", "tensor_tensor", out=s_[:, 224:256], in0=s_[:, 224:256], in1=dtt[sl][:, 0:32], op=ALU.mult, reads=[b_s, b_dtt[sl]], writes=[b_s])
                P.op("act", "activation", out=s_[:, 256:288], in_=s_[:, 64:96], func=AF.Exp, reads=[b_s], writes=[b_s])
                P.op("act", "activation", out=s_[:, 96:160], in_=s_[:, 0:64], func=AF.Exp, reads=[b_s], writes=[b_s])
                P.op("dve", "tensor_tensor", out=v3(xdt[2][:]), in0=v3(xsk[sl][:]), in1=s_[:, 224:256].unsqueeze(2).to_broadcast([128, 32, 64]),
                     op=ALU.mult, reads=[b_xsk[sl], b_s], writes=[b_xdt[2]])
                P.op("act", "activation", out=stack[sl][:], in_=pt2[:, 0:128], func=AF.Copy, reads=[pb2], writes=[b_stack[sl]])
                P.op("dve", "tensor_tensor", out=lo32[64:128, :], in0=pt2[64:128, 0:128], in1=stack[sl][64:128, :], op=ALU.subtract,
                     reads=[pb2, b_stack[sl]], writes=[b_lo32])
                P.op("dve", "tensor_copy", out=stack[sl][64:128, :], in_=lo32[64:128, :], reads=[b_lo32], writes=[b_stack[sl]])
                P.op("dve", "tensor_scalar", out=nstack[sl][:], in0=stack[sl][:], scalar1=-1.0, scalar2=None, op0=ALU.mult,
                     reads=[b_stack[sl]], writes=[b_nstack[sl]])
                for q in range(4):
                    pb, pt = self.ps.one()
                    for gi in range(2):
                        g = 2 * q + gi
                        P.op("pe", "matmul", pt[:, gi * 256:(gi + 1) * 256], Btk[sl][:, g * 128:(g + 1) * 128], xdt[2][:, g * 256:(g + 1) * 256],
                             start=True, stop=True, reads=[b_Btk[sl], b_xdt[2]], writes=[pb])
                    hv = hf[:, q * 512:(q + 1) * 512]
                    P.op("dve", "tensor_tensor", out=v3(ht[q % 2][:]), in0=v3(hv), in1=s_[:, 256 + 8 * q:256 + 8 * q + 8].unsqueeze(2).to_broadcast([128, 8, 64]),
                         op=ALU.mult, reads=[b_hf[q], b_s], writes=[b_ht[q % 2]])
                    P.op("dve", "tensor_tensor", out=hv, in0=ht[q % 2][:], in1=pt, op=ALU.add, reads=[b_ht[q % 2], pb], writes=[b_hf[q]])
                P.op("act", "activation", out=hfb[nxt_h][:], in_=hf[:], func=AF.Copy, reads=b_hf, writes=[b_hfb[nxt_h]])
                yield
                for b2 in range(2):
                    pb, pt = self.ps.one()
                    for gi in range(4):
                        g = 4 * b2 + gi
                        P.op("pe", "matmul", pt[:, gi * 128:(gi + 1) * 128], BT[sl][:, g, :], CT[sl][:, g, :], start=True, stop=True,
                             reads=[b_BT[sl], b_CT[sl]], writes=[pb])
                    pv = pt.rearrange("p (g i) -> p g i", i=128)
                    P.op("act", "activation", out=cbm[sl][:, 4 * b2:4 * b2 + 4, :], in_=pv, func=AF.Copy, reads=[pb], writes=[b_cbm[sl]])
                for k_ in range(2):
                    P.op("dve" if (k_ == 0 or XV & 1) else "pool", "tensor_tensor", out=v3(xdt[k_][:]), in0=v3(xsk[sl][:]),
                         in1=dtt[sl][:, 32 * k_:32 * k_ + 32].unsqueeze(2).to_broadcast([128, 32, 64]), op=ALU.mult,
                         reads=[b_xsk[sl], b_dtt[sl]], writes=[b_xdt[k_]])
                yield
                for hf_ in range(2):
                    n_m = 0
                    for dr in range(2):
                        for gl in range(4):
                            g = 4 * hf_ + gl
                            pb, pt = self.ps.one()
                            nk = "negF" if dr == 0 else "negB"
                            P.op("pe", "matmul", pt, self.idb[:], K[nk][:].rearrange("p a b -> p (a b)"), start=True, stop=False,
                                 reads=[self.b_idb, K["b_" + nk]], writes=[pb])
                            for k_ in range(4):
                                hh = dr * 32 + g * 4 + k_
                                o = pt[:, k_ * 128:(k_ + 1) * 128]
                                P.op("pe", "matmul", o, sel[:, hh, :], stack[sl][:], start=False, stop=False,
                                     reads=[b_sel, b_stack[sl]], writes=[pb])
                                P.op("pe", "matmul", o, nstack[sl][:], sel[:, hh, :], start=False, stop=(k_ == 3),
                                     reads=[b_sel, b_nstack[sl]], writes=[pb])
                            kd = dring.next()
                            P.op("act", "activation", out=dec[kd][:].rearrange("p a b -> p (a b)"), in_=pt, func=AF.Exp, reads=[pb], writes=[b_dec[kd]])
                            P.op("pool" if (n_m % 4 == 3 and not XV & 2) else "dve", "tensor_tensor", out=M[dr][:, 4 * gl:4 * gl + 4, :], in0=dec[kd][:],
                                 in1=cbm[sl][:, g:g + 1, :].to_broadcast([128, 4, 128]), op=ALU.mult,
                                 reads=[b_dec[kd], b_cbm[sl]], writes=[b_M[dr][gl]])
                            n_m += 1
                    yield
                    pbd, ptd = self.ps.two()
                    for hl in range(16):
                        h = 16 * hf_ + hl
                        o = ptd[:, hl // 8, (hl % 8) * 64:(hl % 8 + 1) * 64]
                        P.op("pe", "matmul", o, M[0][:, hl, :], xdt[0][:, h * 64:(h + 1) * 64], start=True, stop=False,
                             reads=[b_M[0][hl // 4], b_xdt[0]], writes=[pbd[hl // 8]])
                        P.op("pe", "matmul", o, M[1][:, hl, :], xdt[1][:, h * 64:(h + 1) * 64], start=False, stop=False,
                             reads=[b_M[1][hl // 4], b_xdt[1]], writes=[pbd[hl // 8]])
                        P.op("pe", "matmul", o, Did[:, h, :], xsk[sl][:, h * 64:(h + 1) * 64], start=False, stop=True,
                             reads=[b_Did, b_xsk[sl]], writes=[pbd[hl // 8]])
                    pbf, ptf = self.ps.two()
                    pbb, ptb_ = self.ps.two()
                    for gl in range(4):
                        g = 4 * hf_ + gl
                        P.op("pe", "matmul", ptf[:, gl // 2, (gl % 2) * 256:(gl % 2 + 1) * 256], CT[sl][:, g, :], hfb[cur][:, g * 256:(g + 1) * 256],
                             start=True, stop=True, reads=[b_CT[sl], b_hfb[cur]], writes=[pbf[gl // 2]])
                        P.op("pe", "matmul", ptb_[:, gl // 2, (gl % 2) * 256:(gl % 2 + 1) * 256], CT[sl][:, g, :], hbc[sl][:, g * 256:(g + 1) * 256],
                             start=True, stop=True, reads=[b_CT[sl], b_hbc[sl]], writes=[pbb[gl // 2]])
                    Ef = s_[:, 96 + 16 * hf_:96 + 16 * hf_ + 16].unsqueeze(2).to_broadcast([128, 16, 64])
                    Eb = s_[:, 128 + 16 * hf_:128 + 16 * hf_ + 16].unsqueeze(2).to_broadcast([128, 16, 64])
                    P.op("dve", "tensor_tensor", out=v3(tA[:]), in0=v3(ptf.rearrange("p a b -> p (a b)")), in1=Ef, op=ALU.mult,
                         reads=pbf + [b_s], writes=[b_tA])
                    P.op("dve", "tensor_tensor", out=tA[:], in0=tA[:], in1=ptd.rearrange("p a b -> p (a b)"), op=ALU.add,
                         reads=pbd + [b_tA], writes=[b_tA])
                    P.op("dve", "tensor_tensor", out=v3(tB[:]), in0=v3(ptb_.rearrange("p a b -> p (a b)")), in1=Eb, op=ALU.mult,
                         reads=pbb + [b_s], writes=[b_tB])
                    yh = y[:, hf_ * 1024:(hf_ + 1) * 1024]
                    P.op("dve", "tensor_tensor", out=yh, in0=tA[:], in1=tB[:], op=ALU.add, reads=[b_tA, b_tB], writes=[b_y[hf_]])
                    P.op("pool", "tensor_tensor", out=yh, in0=yh, in1=zs[sl][:, hf_ * 1024:(hf_ + 1) * 1024], op=ALU.mult,
                         reads=[b_y[hf_], b_zs[sl]], writes=[b_y[hf_]])
                    for gl in range(4):
                        g = 4 * hf_ + gl
                        P.op("act", "activation", out=c["junk"][:, gl * 256:(gl + 1) * 256], in_=y[:, g * 256:(g + 1) * 256], func=AF.Square,
                             accum_out=s_[:, 288 + g:289 + g], reads=[b_y[hf_]], writes=[c["b_junk"], b_s])
                    yield
                if LNEXP:
                    P.op("act", "activation", out=s_[:, 296:304], in_=s_[:, 288:296], func=AF.Ln, scale=1.0 / 256, bias=EPS, reads=[b_s], writes=[b_s])
                    P.op("act", "activation", out=s_[:, 304:312], in_=s_[:, 296:304], func=AF.Exp, scale=-0.5, reads=[b_s], writes=[b_s])
                else:
                    P.op("act", "activation", out=s_[:, 296:304], in_=s_[:, 288:296], func=AF.Sqrt, scale=1.0 / 256, bias=EPS, reads=[b_s], writes=[b_s])
                    P.op("dve", "reciprocal", out=s_[:, 304:312], in_=s_[:, 296:304], reads=[b_s], writes=[b_s])
                P.op("dve", "tensor_tensor", out=yn[:].rearrange("p (g e) -> p g e", e=256), in0=y[:].rearrange("p (g e) -> p g e", e=256),
                     in1=s_[:, 304:312].unsqueeze(2).to_broadcast([128, 8, 256]), op=ALU.mult, reads=b_y + [b_s], writes=[b_yn])
                for hh2 in range(2):
                    pb, pt = self.ps.one()
                    ptb16 = pt.bitcast(BF16).rearrange("p (c t) -> p c t", t=128)
                    for q in range(8):
                        j = hh2 * 8 + q
                        P.op("pe", "transpose", ptb16[:, q, :], yn[:, j * 128:(j + 1) * 128], self.idb[:], reads=[b_yn, self.b_idb], writes=[pb])
                    if hh2 == 0:
                        P.op("act", "activation", out=ynT[:, hh2 * 8:(hh2 + 1) * 8, :], in_=ptb16, func=AF.Copy, reads=[pb], writes=[b_ynT[hh2]])
                    else:
                        P.op("dve", "tensor_copy", out=ynT[:, hh2 * 8:(hh2 + 1) * 8, :], in_=ptb16, reads=[pb], writes=[b_ynT[hh2]])
                yield
                pbs, pto = self.ps.two()
                for h2 in range(2):
                    for j in range(16):
                        P.op("pe", "matmul", pto[:, h2, :], ynT[:, j, :], wo3[:, j, h2 * 512:(h2 + 1) * 512], start=(j == 0), stop=(j == 15),
                             reads=[b_ynT[j // 8], b_wo3], writes=[pbs[h2]])
                self.epilogue(c, pbs, pto, c["xt"][sl][0], c["b_xt"][sl][0], mod, v)
                self.store_tile(c, sl, R, t0, 128)

            gens = []
            it = 0
            for (nm, R, L, v) in seqs:
                for ci in range(L // 128):
                    gens.append(chunk(it, scr[nm], R, ci, v))
                    it += 1
            run_pipeline(gens, 3 if XV & 4 else 4)
            P.barrier()
            P.release(mk)

    def layer(self, l):
        I = self.I
        last = (l == 1)
        first_src_x = I["x"] if l == 0 else self.R
        first_src_c = I["ctx"] if l == 0 else self.Rc
        st0 = [(first_src_x, self.R, self.S, 0)]
        st1 = [(self.R, self.out if last else self.R, self.S, 0)]
        if not last:
            st0.append((first_src_c, self.Rc, self.CL, 1))
            st1.append((self.Rc, self.Rc, self.CL, 1))
        if self.on("ffn%d0" % l):
            self.ffn(l, 0, st0, nxt=(0, 1) if l == 0 else (1, 1))
        elif l == 0:
            P = self.P
            b_rc = P.buf(None, "rcopy")
            for t0 in range(0, self.S, 512):
                P.dma("sp", self.R[t0:t0 + 512, :], I["x"][t0:t0 + 512, :], writes=[b_rc])
            P.dma("sp", self.Rc[:, :], I["ctx"][:, :], writes=[b_rc])
            P.barrier()
        if l == 0 and self.on("ssd"):
            self.ssd(l)
        if l == 1 and self.on("gmlp"):
            self.gmlp(l, self.R, self.S)
        if self.on("ffn%d1" % l):
            self.ffn(l, 1, st1)
        elif last:
            P = self.P
            b_fin = P.buf(None, "fin")
            for t0 in range(0, self.S, 512):
                P.dma("sp", self.out[t0:t0 + 512, :], self.R[t0:t0 + 512, :], writes=[b_fin])
            P.barrier()


def host_inputs(inp, b):
    f = np.float32
    m = {}
    m["x"] = np.ascontiguousarray(inp["x"][b])
    m["ctx"] = np.ascontiguousarray(inp["ctx"][b])
    cc = np.stack([inp["c"][b].reshape(8, 128).T, inp["c_ctx"].reshape(8, 128).T], axis=-1)
    m["c_col"] = np.ascontiguousarray(cc.astype(f))
    m["ada_w"] = inp["ada_w"]
    m["ada_b"] = inp["ada_b"]
    m["ada_b_col"] = np.ascontiguousarray(inp["ada_b"].reshape(2, 72, 128).transpose(0, 2, 1))
    m["norm_g"] = inp["norm_g"]
    m["norm_g_col"] = np.ascontiguousarray(inp["norm_g"].reshape(2, 6, 8, 128).transpose(0, 3, 1, 2))
    wi = inp["ffn_w_in"].reshape(2, 2, 8, 128, 2, NJ, 128)
    m["ffn_wi"] = np.ascontiguousarray(wi.transpose(0, 1, 5, 3, 4, 2, 6)).reshape(2, 2, NJ, 128, 2 * 8 * 128)
    wo = inp["ffn_w_out"].reshape(2, 2, NJ, 128, D)
    m["ffn_wo"] = np.ascontiguousarray(wo.transpose(0, 1, 3, 2, 4)).reshape(2, 2, 128, NJ * D)
    m["ssd_w_in"] = inp["ssd_w_in"][0]
    m["ssd_cw"] = np.ascontiguousarray(inp["ssd_conv_w"][0].reshape(5, 32, 128).transpose(2, 1, 0))
    m["ssd_cb"] = np.ascontiguousarray(inp["ssd_conv_b"][0].reshape(32, 128).T)
    m["ssd_dt_bias"] = np.ascontiguousarray(inp["ssd_dt_bias"][0].reshape(1, 64))
    m["ssd_A_log"] = np.ascontiguousarray(inp["ssd_A_log"][0].reshape(1, 64))
    m["ssd_D"] = inp["ssd_D"]
    m["ssd_ng_col"] = np.ascontiguousarray(inp["ssd_norm_g"][0].reshape(16, 128).T)
    m["ssd_w_out"] = inp["ssd_w_out"][0]
    m["gm_w_in"] = inp["gm_w_in"][0]
    m["gm_w_out"] = inp["gm_w_out"][0]
    m["gm_wsT"] = np.ascontiguousarray(inp["gm_w_s"][0].transpose(2, 0, 1))
    m["gm_bsT"] = np.ascontiguousarray(inp["gm_b_s"][0].T)
    m["gm_v_g"] = inp["gm_v_g"]
    m["gm_v_b"] = inp["gm_v_b"]
    return m


_CACHE = {}


def kernel(**inputs):
    inp = {k: np.asarray(v) for k, v in inputs.items()}
    B, S, _ = inp["x"].shape
    CL = inp["ctx"].shape[1]
    key = (S, CL)
    if key not in _CACHE:
        _CACHE[key] = Builder(S, CL).build()
    nc = _CACHE[key]
    shared = None
    in_maps = []
    for b in range(B):
        m = host_inputs(inp, b)
        if shared is None:
            shared = m
        else:
            for k in m:
                if k not in ("x", "ctx", "c_col"):
                    m[k] = shared[k]
        in_maps.append(m)
    res = run_bass_kernel_spmd(nc, in_maps, core_ids=list(range(B)))
    return np.stack([np.asarray(r["out"]) for r in res.results], axis=0).astype(np.float32)
```
